# Optimizing a Trainium2 kernel written in Bass

```python
import jax, jax.numpy as jnp
from jax import lax
import numpy as np

D_MODEL = 2048
BATCH = 8
SEQ = 4096
DEPTH = 1
DEC_BATCH = 32
DEC_SEQ = 64
PAST_LEN = 1024

CHUNK = 64
PAST_CHUNKS = 8
KV_REACH = PAST_CHUNKS * CHUNK
N_HEADS = 16
HEAD_DIM = 64
D_ATTN = N_HEADS * HEAD_DIM
MAX_REL = 256
D_RNN = D_MODEL
N_RNN_BLOCKS = 16
RNN_BLOCK = D_RNN // N_RNN_BLOCKS
CONV_WIDTH = 4
LRU_C = 8.0
D_FF = 4 * D_MODEL
N_IN = 3 * D_ATTN + 2 * D_RNN + 2 * D_MODEL
EPS = 1e-6
NEG_INF = -1e30

kernel_name = 'hybrid_chunk_attn_rglru_step'


def rms_norm(x, g):
    xf = x.astype(jnp.float32)
    y = xf * lax.rsqrt(jnp.mean(xf * xf, axis=-1, keepdims=True) + EPS)
    return (y * g.astype(jnp.float32)).astype(x.dtype)


def rel_pos_bias(rel_bias, dist):
    return rel_bias[:, jnp.clip(dist, -MAX_REL, MAX_REL) + MAX_REL]


def attend(q, k, v, bias, valid):
    s = jnp.einsum('bqhd,bkhd->bhqk', q, k).astype(jnp.float32) * (HEAD_DIM ** -0.5)
    s = s + bias.astype(jnp.float32)[None]
    if valid is not None:
        s = jnp.where(valid, s, NEG_INF)
    p = jax.nn.softmax(s, axis=-1).astype(v.dtype)
    return jnp.einsum('bhqk,bkhd->bqhd', p, v)


def chunk_band_attention(q, k, v, rel_bias):
    b, s, h, dh = q.shape
    n_chunks = s // CHUNK
    band = KV_REACH + CHUNK
    pad = jnp.zeros((b, KV_REACH, h, dh), k.dtype)
    kp = jnp.concatenate([pad, k], axis=1)
    vp = jnp.concatenate([pad, v], axis=1)
    qi = jnp.arange(CHUNK)[:, None]
    ku = jnp.arange(band)[None, :]
    bias = rel_pos_bias(rel_bias, qi + KV_REACH - ku)

    def one_chunk(c):
        start = c * CHUNK
        qc = lax.dynamic_slice_in_dim(q, start, CHUNK, axis=1)
        kc = lax.dynamic_slice_in_dim(kp, start, band, axis=1)
        vc = lax.dynamic_slice_in_dim(vp, start, band, axis=1)
        valid = jnp.broadcast_to(ku >= KV_REACH - start, (CHUNK, band))
        return attend(qc, kc, vc, bias, valid)

    out = lax.map(one_chunk, jnp.arange(n_chunks))
    return jnp.swapaxes(out, 0, 1).reshape(b, s, h, dh)


def cached_band_attention(q, k_new, v_new, k_cache, v_cache, rel_bias):
    t = q.shape[1]
    n_cached = k_cache.shape[1]
    k = jnp.concatenate([k_cache.astype(k_new.dtype), k_new], axis=1)
    v = jnp.concatenate([v_cache.astype(v_new.dtype), v_new], axis=1)
    dist = jnp.arange(t)[:, None] + n_cached - jnp.arange(n_cached + t)[None, :]
    return attend(q, k, v, rel_pos_bias(rel_bias, dist), None)


def causal_depthwise_conv(x_ext, w, bias, t):
    out = bias + w[0] * x_ext[:, 0:t]
    for j in range(1, CONV_WIDTH):
        out = out + w[j] * x_ext[:, j:j + t]
    return out


def block_diag_linear(x, w, b):
    xb = x.reshape(x.shape[:-1] + (N_RNN_BLOCKS, RNN_BLOCK))
    return jnp.einsum('btnc,ncd->btnd', xb, w).reshape(x.shape) + b


def rg_lru(x, w_rg, b_rg, w_ig, b_ig, lru_lambda, h0):
    f32 = jnp.float32
    r = jax.nn.sigmoid(block_diag_linear(x, w_rg, b_rg).astype(f32))
    i = jax.nn.sigmoid(block_diag_linear(x, w_ig, b_ig).astype(f32))
    log_a = LRU_C * r * jax.nn.log_sigmoid(lru_lambda.astype(f32))
    a = jnp.exp(log_a)
    u = jnp.sqrt(-jnp.expm1(2.0 * log_a)) * (i * x.astype(f32))
    if h0 is not None:
        u = u.at[:, 0].add(a[:, 0] * h0.astype(f32))

    def combine(left, right):
        a_l, u_l = left
        a_r, u_r = right
        return a_l * a_r, a_r * u_l + u_r

    _, h = lax.associative_scan(combine, (a, u), axis=1)
    return h


def layer(x, cache_k, cache_v, conv_state, h0,
          pre_mix_g, w_in, rel_bias, conv_w, conv_b, w_rg, b_rg, w_ig, b_ig, lru_lambda,
          w_attn_up, w_rnn_up, w_out, post_mix_g, pre_ffn_g, w_ff1, w_ff2, post_ffn_g):
    b, t, _ = x.shape
    xn = rms_norm(x, pre_mix_g)
    z = xn @ w_in
    splits = [D_ATTN, 2 * D_ATTN, 3 * D_ATTN, 3 * D_ATTN + D_RNN,
              3 * D_ATTN + 2 * D_RNN, 3 * D_ATTN + 2 * D_RNN + D_MODEL]
    q, k, v, xr, gr, ga, gb = jnp.split(z, splits, axis=-1)
    q = q.reshape(b, t, N_HEADS, HEAD_DIM)
    k = k.reshape(b, t, N_HEADS, HEAD_DIM)
    v = v.reshape(b, t, N_HEADS, HEAD_DIM)

    if cache_k is None:
        o_a = chunk_band_attention(q, k, v, rel_bias)
        keep = min(KV_REACH, t)
        new_k, new_v = k[:, t - keep:], v[:, t - keep:]
        conv_prefix = jnp.zeros((b, CONV_WIDTH - 1, D_RNN), xr.dtype)
    else:
        o_a = cached_band_attention(q, k, v, cache_k, cache_v, rel_bias)
        new_k, new_v = k, v
        conv_prefix = conv_state.astype(xr.dtype)

    xr_ext = jnp.concatenate([conv_prefix, xr], axis=1)
    xc = causal_depthwise_conv(xr_ext, conv_w, conv_b, t)
    h = rg_lru(xc, w_rg, b_rg, w_ig, b_ig, lru_lambda, h0)
    o_b = (h * jax.nn.gelu(gr.astype(jnp.float32), approximate=True)).astype(x.dtype)

    merged = (jax.nn.sigmoid(ga) * (o_a.reshape(b, t, D_ATTN) @ w_attn_up)
              + jax.nn.sigmoid(gb) * (o_b @ w_rnn_up))
    x = x + rms_norm(merged @ w_out, post_mix_g)

    hidden = jnp.square(jax.nn.relu(rms_norm(x, pre_ffn_g) @ w_ff1))
    x = x + rms_norm(hidden @ w_ff2, post_ffn_g)
    return x, new_k, new_v, xr_ext[:, t:], h[:, -1].astype(x.dtype)


def setup_inputs(seed: int = 0) -> dict:
    key = jax.random.key(seed)
    ks = jax.random.split(key, 24)
    kv_len = min(KV_REACH, PAST_LEN)

    def nrm(k, shape, scale):
        return jax.random.normal(k, shape, jnp.float32) * scale

    def gain(k):
        return 1.0 + nrm(k, (DEPTH, D_MODEL), 0.05)

    a_target = jax.random.uniform(ks[15], (DEPTH, D_RNN), jnp.float32, 0.9, 0.999)
    s = a_target ** (1.0 / LRU_C)
    lru_lambda = jnp.log(s) - jnp.log1p(-s)

    return {
        'x_prompt': nrm(ks[0], (BATCH, SEQ, D_MODEL), 1.0),
        'x_sample': nrm(ks[1], (DEC_BATCH, DEC_SEQ, D_MODEL), 1.0),
        'cache_k': nrm(ks[2], (DEPTH, DEC_BATCH, kv_len, N_HEADS, HEAD_DIM), 1.0),
        'cache_v': nrm(ks[3], (DEPTH, DEC_BATCH, kv_len, N_HEADS, HEAD_DIM), 1.0),
        'state_conv': nrm(ks[4], (DEPTH, DEC_BATCH, CONV_WIDTH - 1, D_RNN), 1.0),
        'state_h': nrm(ks[5], (DEPTH, DEC_BATCH, D_RNN), 1.0),
        'pre_mix_g': gain(ks[6]),
        'w_in': nrm(ks[7], (DEPTH, D_MODEL, N_IN), D_MODEL ** -0.5),
        'rel_bias': nrm(ks[8], (DEPTH, N_HEADS, 2 * MAX_REL + 1), 0.5),
        'conv_w': nrm(ks[9], (DEPTH, CONV_WIDTH, D_RNN), CONV_WIDTH ** -0.5),
        'conv_b': nrm(ks[10], (DEPTH, D_RNN), 0.02),
        'w_rg': nrm(ks[11], (DEPTH, N_RNN_BLOCKS, RNN_BLOCK, RNN_BLOCK), RNN_BLOCK ** -0.5),
        'b_rg': nrm(ks[12], (DEPTH, D_RNN), 0.02),
        'w_ig': nrm(ks[13], (DEPTH, N_RNN_BLOCKS, RNN_BLOCK, RNN_BLOCK), RNN_BLOCK ** -0.5),
        'b_ig': nrm(ks[14], (DEPTH, D_RNN), 0.02),
        'lru_lambda': lru_lambda,
        'w_attn_up': nrm(ks[16], (DEPTH, D_ATTN, D_MODEL), D_ATTN ** -0.5),
        'w_rnn_up': nrm(ks[17], (DEPTH, D_RNN, D_MODEL), D_RNN ** -0.5),
        'w_out': nrm(ks[18], (DEPTH, D_MODEL, D_MODEL), D_MODEL ** -0.5),
        'post_mix_g': gain(ks[19]),
        'pre_ffn_g': gain(ks[20]),
        'w_ff1': nrm(ks[21], (DEPTH, D_MODEL, D_FF), D_MODEL ** -0.5),
        'w_ff2': nrm(ks[22], (DEPTH, D_FF, D_MODEL), D_FF ** -0.5),
        'post_ffn_g': gain(ks[23]),
    }


def reference(x_prompt, x_sample, cache_k, cache_v, state_conv, state_h,
              pre_mix_g, w_in, rel_bias, conv_w, conv_b, w_rg, b_rg, w_ig, b_ig, lru_lambda,
              w_attn_up, w_rnn_up, w_out, post_mix_g, pre_ffn_g, w_ff1, w_ff2, post_ffn_g):
    y_prompt, y_sample = x_prompt, x_sample
    kp_l, vp_l, cp_l, hp_l = [], [], [], []
    ks_l, vs_l, cs_l, hs_l = [], [], [], []
    for l in range(DEPTH):
        lw = (pre_mix_g[l], w_in[l], rel_bias[l], conv_w[l], conv_b[l], w_rg[l], b_rg[l],
              w_ig[l], b_ig[l], lru_lambda[l], w_attn_up[l], w_rnn_up[l], w_out[l],
              post_mix_g[l], pre_ffn_g[l], w_ff1[l], w_ff2[l], post_ffn_g[l])
        y_prompt, kp, vp, cp, hp = layer(y_prompt, None, None, None, None, *lw)
        y_sample, ksm, vsm, csm, hsm = layer(y_sample, cache_k[l], cache_v[l],
                                             state_conv[l], state_h[l], *lw)
        kp_l.append(kp); vp_l.append(vp); cp_l.append(cp); hp_l.append(hp)
        ks_l.append(ksm); vs_l.append(vsm); cs_l.append(csm); hs_l.append(hsm)
    k_prompt = jnp.stack(kp_l)
    v_prompt = jnp.stack(vp_l)
    conv_prompt = jnp.stack(cp_l)
    h_prompt = jnp.stack(hp_l)
    k_sample = jnp.stack(ks_l)
    v_sample = jnp.stack(vs_l)
    conv_sample = jnp.stack(cs_l)
    h_sample = jnp.stack(hs_l)
    return (y_prompt, y_sample, k_prompt, v_prompt, conv_prompt, h_prompt,
            k_sample, v_sample, conv_sample, h_sample)
```

```python
import numpy as np
from contextlib import ExitStack
import concourse.bass as bass
import concourse.mybir as mybir
from concourse.ap import AP
from concourse.bass_utils import run_bass_kernel_spmd

F32 = mybir.dt.float32
BF16 = mybir.dt.bfloat16
AF = mybir.ActivationFunctionType
ALU = mybir.AluOpType

D = 2048
KC = 16
NH = 16
DH = 64
DA = 1024
DFF = 8192
NIN = 11264
EPS = 1e-6
C_Q, C_K, C_V, C_XR, C_GR, C_GA, C_GB = 0, 1024, 2048, 3072, 5120, 7168, 9216
NB_W = 5
K_DMA = 8


class Op:
    __slots__ = ("eng", "fn", "reads", "writes", "dma", "deps", "sem", "val", "need")

    def __init__(self, eng, fn, reads, writes, dma):
        self.eng = eng
        self.fn = fn
        self.reads = reads
        self.writes = writes
        self.dma = dma
        self.deps = ()
        self.sem = None
        self.val = 0
        self.need = False


class Sched:
    def __init__(self):
        self.ops = []
        self.ranges = {}

    def add(self, eng, fn, reads=(), writes=(), dma=False):
        self.ops.append(Op(eng, fn, tuple(reads), tuple(writes), dma))

    def set_range(self, key, region, lo, hi):
        self.ranges[key] = (region, lo, hi)

    def analyse(self):
        last_w = {}
        readers = {}
        pages = {}
        ops = self.ops

        def live_add(k):
            r = self.ranges.get(k)
            if r is None:
                return
            reg, lo, hi = r
            for p in range(lo // 1024, (hi + 1023) // 1024):
                pages.setdefault((reg, p), set()).add(k)

        def overlapping(k):
            r = self.ranges.get(k)
            if r is None:
                return ()
            reg, lo, hi = r
            out = set()
            for p in range(lo // 1024, (hi + 1023) // 1024):
                for k2 in pages.get((reg, p), ()):
                    if k2 != k:
                        r2 = self.ranges[k2]
                        if r2[1] < hi and lo < r2[2]:
                            out.add(k2)
            return out

        def want(i, j, raw):
            a, b = ops[j], ops[i]
            if a.dma or b.dma:
                return True
            if a.eng != b.eng:
                return True
            if a.eng == "pe":
                return False
            return raw

        for i, op in enumerate(ops):
            deps = set()
            for k in op.reads:
                j = last_w.get(k)
                if j is not None and want(i, j, True):
                    deps.add(j)
                if k[0] == "ps":
                    for j in readers.get(k, ()):
                        if ops[j].eng != op.eng:
                            deps.add(j)
            for k in op.writes:
                j = last_w.get(k)
                if j is not None and want(i, j, False):
                    deps.add(j)
                for j in readers.get(k, ()):
                    if j != i and want(i, j, False):
                        deps.add(j)
                for k2 in overlapping(k):
                    j = last_w.get(k2)
                    if j is not None and want(i, j, False):
                        deps.add(j)
                    for j in readers.get(k2, ()):
                        if want(i, j, False):
                            deps.add(j)
            for k in op.writes:
                last_w[k] = i
                readers[k] = []
                live_add(k)
            for k in op.reads:
                readers.setdefault(k, []).append(i)
                live_add(k)
            deps.discard(i)
            op.deps = tuple(sorted(deps))
            for j in op.deps:
                ops[j].need = True

    def emit(self, nc, es):
        self.analyse()
        ops = self.ops
        engs = ["pe", "act", "dve", "pool", "sp"]
        esem = {e: es.enter_context(nc.semaphore("sem_" + e)) for e in engs if e != "sp"}
        dq = {q: [es.enter_context(nc.semaphore("dq_%s_%d" % (q, i))) for i in range(K_DMA)]
              for q in ("sp", "pool", "act")}
        cnt = {e: 0 for e in engs}
        dcnt = {q: 0 for q in dq}
        prev_use = {}
        for i, op in enumerate(ops):
            if op.dma:
                j = dcnt[op.eng]
                dcnt[op.eng] += 1
                op.sem = dq[op.eng][j % K_DMA]
                op.val = 16 * (j // K_DMA + 1)
                prev_use[i] = (op.sem, op.val - 16) if j >= K_DMA else None
            elif op.need:
                cnt[op.eng] += 1
                op.sem = esem[op.eng]
                op.val = cnt[op.eng]
        final = {q: [(dq[q][s], 16 * ((dcnt[q] - 1 - s) // K_DMA + 1)) for s in range(K_DMA) if dcnt[q] > s]
                 for q in dq}
        block = es.enter_context(nc.Block())

        def run(engname):
            def body(e):
                waited = {}

                def wait(sem, val):
                    if val <= 0:
                        return
                    key = id(sem)
                    if waited.get(key, 0) >= val:
                        return
                    waited[key] = val
                    e.wait_ge(sem, val)

                for i, op in enumerate(ops):
                    if op.eng != engname:
                        continue
                    for j in op.deps:
                        p = ops[j]
                        wait(p.sem, p.val)
                    if op.dma and prev_use[i] is not None:
                        wait(*prev_use[i])
                    inst = op.fn(e)
                    if op.dma:
                        inst.then_inc(op.sem, 16)
                    elif op.need:
                        inst.then_inc(op.sem, 1)
                for (sem, val) in final.get(engname, ()):
                    wait(sem, val)
            return body

        block.tensor(run("pe"))
        block.scalar(run("act"))
        block.vector(run("dve"))
        block.gpsimd(run("pool"))
        block.sync(run("sp"))


def build_program(n_pt=8, kstop=None):
    import os
    kstop = kstop or os.environ.get('KSTOP')
    halted = [False]
    nc = bass.Bass("TRN2", target_bir_lowering=False)
    es = ExitStack()
    S = Sched()
    SP_LEN = n_pt * 512

    def din(name, shape):
        return nc.dram_tensor(name, shape, F32, kind="ExternalInput")

    def dout(name, shape):
        return nc.dram_tensor(name, shape, F32, kind="ExternalOutput")

    x_p = din("x_p", [SP_LEN, D]); x_s = din("x_s", [256, D])
    ck_d = din("ck", [4, 512, DA]); cv_d = din("cv", [4, 512, DA])
    sconv_d = din("sconv", [12, D]); sh_d = din("sh", [4, D])
    vec_names = ["pre_mix_g", "pre_ffn_g", "conv_w0", "conv_w1", "conv_w2", "conv_w3",
                 "conv_b", "b_rg", "b_ig", "lru_lambda"]
    V_PREG, V_FFNG, V_CW0, V_CB, V_BRG, V_BIG, V_LAM = 0, 1, 2, 6, 7, 8, 9
    vecs_d = din("vecs", [10, D])
    post_mix_g_d = din("post_mix_g", [1, D]); post_ffn_g_d = din("post_ffn_g", [1, D])
    w_in_d = din("w_in", [D, NIN]); rel_bias_d = din("rel_bias", [NH, 513])
    w_rg_d = din("w_rg", [16, 128, 128]); w_ig_d = din("w_ig", [16, 128, 128])
    w_au_d = din("w_attn_up", [DA, D]); w_ru_d = din("w_rnn_up", [D, D]); w_out_d = din("w_out", [D, D])
    w_f1_d = din("w_ff1", [D, DFF]); w_f2_d = din("w_ff2", [DFF, D])

    y_p = dout("y_p", [SP_LEN, D]); y_s = dout("y_s", [256, D])
    k_p = dout("k_p", [512, DA]); v_p = dout("v_p", [512, DA])
    conv_p = dout("conv_p", [3, D]); h_p = dout("h_p", [1, D])
    k_s = dout("k_s", [256, DA]); v_s = dout("v_s", [256, DA])
    conv_s = dout("conv_s", [12, D]); h_s = dout("h_s", [4, D])

    units = {}
    unit_src = []

    def add_unit(name, kind, srcs):
        units[name] = (len(unit_src), kind)
        unit_src.append((name, kind, srcs))

    def wslice(w, k0, nk, c0, ncol):
        return w.ap()[k0 * 128:(k0 + nk) * 128, c0:c0 + ncol].rearrange("(kc p) n -> p kc n", p=128)

    for i in range(4):
        add_unit("q%d" % i, "a", [wslice(w_in_d, 0, 16, C_Q + i * 256, 256)])
    for i in range(4):
        add_unit("k%d" % i, "a", [wslice(w_in_d, 0, 16, C_K + i * 256, 256)])
    for cg in range(2):
        for kh in range(2):
            add_unit("v%d%d" % (cg, kh), "b", [wslice(w_in_d, kh * 8, 8, C_V + cg * 512, 512)])
    for cg in range(2):
        for kh in range(2):
            add_unit("kb%d%d" % (cg, kh), "b", [wslice(w_in_d, kh * 8, 8, C_K + cg * 512, 512)])
    add_unit("gates", "g", [w_rg_d.ap().rearrange("n c d -> c n d"), w_ig_d.ap().rearrange("n c d -> c n d")])
    for i in range(8):
        add_unit("xr%d" % i, "a", [wslice(w_in_d, 0, 16, C_XR + i * 256, 256)])
        add_unit("gr%d" % i, "a", [wslice(w_in_d, 0, 16, C_GR + i * 256, 256)])
    for i in range(8):
        add_unit("ga%d" % i, "a", [wslice(w_in_d, 0, 16, C_GA + i * 256, 256)])
        add_unit("gb%d" % i, "a", [wslice(w_in_d, 0, 16, C_GB + i * 256, 256)])
        add_unit("au%d" % i, "h", [wslice(w_au_d, 0, 8, i * 256, 256)])
        add_unit("ru%d" % i, "a", [wslice(w_ru_d, 0, 16, i * 256, 256)])
    for cg in range(4):
        for kh in range(2):
            add_unit("wo%d%d" % (cg, kh), "b", [wslice(w_out_d, kh * 8, 8, cg * 512, 512)])
    for i in range(32):
        add_unit("f1_%d" % i, "a", [wslice(w_f1_d, 0, 16, i * 256, 256)])
    for cg in range(4):
        for pc in range(8):
            add_unit("f2_%d%d" % (cg, pc), "b", [wslice(w_f2_d, pc * 8, 8, cg * 512, 512)])
    NU = len(unit_src)
    wsc = nc.dram_tensor("wsc", [NU, 128, 4096], BF16, kind="Internal")
    rbx = nc.dram_tensor("rbx", [NH, 1024], F32, kind="Internal")
    ebd = nc.dram_tensor("ebd", [128, NH * 640], BF16, kind="Internal")

    def wsc_view(u, kind):
        base = wsc.ap()[u]
        if kind == "a":
            return base.rearrange("p (k n) -> p k n", k=16)
        if kind == "b":
            return base.rearrange("p (k n) -> p k n", k=8)
        if kind == "h":
            return base[:, 0:2048].rearrange("p (k n) -> p k n", k=8)
        if kind == "g":
            return base.rearrange("p (k n) -> p k n", k=16)
        raise ValueError(kind)

    def sb(name, shape, dt):
        return es.enter_context(nc.sbuf_tensor(name, shape, dt))

    R = sb("R", [128, 32768], BF16)
    R2 = sb("R2", [128, 8192], BF16)
    x_sb = sb("x_sb", [128, 4, D], F32)
    kT = sb("kT", [128, 8, 1024], BF16)
    Vr = sb("Vr", [128, 8, NH, 65], BF16)
    wbuf = [sb("wbuf%d" % i, [128, 4096], BF16) for i in range(NB_W)]
    rl = sb("rl", [128, 2, 512], F32)
    gbuf = sb("gbuf", [128, 4096], BF16)
    ident = sb("ident", [128, 128], F32)
    identb = sb("identb", [128, 128], BF16)
    vecT = sb("vecT", [128, 16, 16], F32)
    stT = sb("stT", [128, 16, 16], F32)
    dT = sb("dT", [128, 16, 8], F32)
    etmp = sb("etmp", [128, 16, 2], F32)
    hist = sb("hist", [128, 16, 4], F32)
    hlast = sb("hlast", [128, 16, 4], F32)
    outst = sb("outst", [128, 16, 16], F32)
    ss = sb("ss", [128, 8], F32)
    sd = sb("sd", [128, 8], F32)
    rstd = sb("rstd", [128, 8], F32)
    rc = sb("rc", [128, 4], F32)
    PS = [es.enter_context(nc.psum_tensor("PS%d" % i, [128, 2048], F32)) for i in range(2)]

    def bank(b):
        return PS[b // 4][:, (b % 4) * 512:(b % 4 + 1) * 512]

    def pk(b):
        return ("ps", b)

    class RB:
        def __init__(self, name, region, reg_name, off, shape, dt):
            self.name = name
            esz = 4 if dt == F32 else 2
            n = int(np.prod(shape))
            self.off = off
            self.nbytes = n * esz
            base = region[:, off // 2: off // 2 + n * esz // 2]
            if dt == F32:
                base = base.bitcast(F32)
            if len(shape) == 1:
                self.ap = base
            elif len(shape) == 2:
                self.ap = base.rearrange("p (a b) -> p a b", a=shape[0])
            else:
                self.ap = base.rearrange("p (a b c) -> p a b c", a=shape[0], b=shape[1])
            self.sub = self.nbytes // shape[0] if len(shape) >= 2 else self.nbytes
            self.n0 = shape[0] if len(shape) >= 2 else 1
            self.reg_name = reg_name
            for i in range(self.n0):
                S.set_range((name, i), reg_name, off + i * self.sub, off + (i + 1) * self.sub)

        def k(self, i=0):
            return (self.name, i)

        def keys(self):
            return [(self.name, i) for i in range(self.n0)]

    KB = 1024
    xs_b = RB("xs", R, "R", 0, [4, D], F32)
    junk_b = RB("junk", R, "R", 32 * KB, [D], F32)
    xs_b0 = xs_b; junk_b0 = junk_b
    xsG_b = RB("xsG", R, "R", 16 * KB, [4, D], F32)
    junkG_b = RB("junkG", R, "R", 48 * KB, [D], F32)
    qT_b = RB("qT", R, "R", 0, [8, 512], BF16)
    P_b = RB("P", R, "R", 8 * KB, [4, 640], BF16)
    osb_b = RB("osb", R, "R", 16 * KB, [1024], F32)
    oT_b = RB("oT", R, "R", 20 * KB, [8, 512], BF16)
    EB_b = RB("EB", R, "R", 28 * KB, [NH, 640], BF16)
    vst_b = RB("vst", R, "R", 48 * KB, [2, 512], F32)
    ckst_b = RB("ckst", R, "R", 52 * KB, [2, 1024], F32)
    obT_b = RB("obT", R, "R", 28 * KB, [16, 512], BF16)
    t0 = 44 * KB
    xe_b = RB("xe", R, "R", t0, [520], F32)
    xc_b = RB("xc", R, "R", t0 + 2080, [512], F32)
    xcb_b = RB("xcb", R, "R", t0 + 4128, [512], BF16)
    thr_b = RB("thr", R, "R", t0 + 5152, [512], F32)
    a_b = RB("a", R, "R", t0 + 7200, [512], F32)
    a2_b = RB("a2", R, "R", t0 + 9248, [512], F32)
    thi_b = RB("thi", R, "R", t0 + 11296, [512], F32)
    u_b = RB("u", R, "R", t0 + 13344, [512], F32)
    h_b = RB("h", R, "R", t0 + 15392, [512], F32)
    gl_b = RB("gl", R, "R", t0 + 17440, [512], F32)
    sg_b = RB("sg", R, "R", 44 * KB, [4, 512], F32)
    m1_b = RB("m1", R, "R", 52 * KB, [2, 512], F32)
    mT_b = RB("mT", R, "R", 0, [16, 512], BF16)
    hid_b = RB("hid", R, "R", 0, [64, 512], BF16)
    BT_b = RB("BT", R, "R", 0, [NH, 640], F32)
    EBp_b = RB("EBp", R, "R", 40 * KB, [NH, 640], BF16)
    xnT_b = RB("xnT", R2, "R2", 0, [16, 512], BF16)
    VR_b = RB("VRb", R2, "R2", 0, [D], F32)
    SR_b = RB("SRb", R2, "R2", 8 * KB, [D], F32)
    rbs_b = RB("rbs", R, "R", 60 * KB, [1024], F32)
    outrow_b = RB("outrow", R, "R", 0, [D], F32)
    VR = VR_b.ap; SR = SR_b.ap; rb_sb = rbs_b.ap; outrow = outrow_b.ap
    tF_b = RB("tF", R2, "R2", 0, [D], F32)
    gbc_b = RB("gbc", R2, "R2", 8 * KB, [D], F32)

    stream = []

    class WStream:
        def __init__(self):
            self.pos = 0
            self.loaded = 0

        def _record_load(self, j):
            name = stream[j]
            u, kind = units[name]
            b = j % NB_W
            n = 2048 if kind == "h" else 4096
            if name not in direct_done:
                direct_done.add(name)
                kk = 16 if kind == "a" else 8
                dstv = wbuf[b][:, 0:n].rearrange("p (k n) -> p k n", k=kk)
                src = usrc[name][1][0]
                S.add("pool", lambda e, dstv=dstv, src=src: e.dma_start(out=dstv, in_=src),
                      writes=[("wbuf", b)], dma=True)
                S.add("sp", lambda e, u=u, b=b, n=n: e.dma_start(out=wsc.ap()[u][:, 0:n], in_=wbuf[b][:, 0:n]),
                      reads=[("wbuf", b)], writes=[("wsc", u)], dma=True)
                return
            S.add("sp", lambda e, u=u, b=b, n=n: e.dma_start(out=wbuf[b][:, 0:n], in_=wsc.ap()[u][:, 0:n]),
                  reads=[("wsc", u)], writes=[("wbuf", b)], dma=True)

        def prefetch(self):
            while self.loaded < len(stream) and self.loaded < self.pos + NB_W - 1:
                self._record_load(self.loaded)
                self.loaded += 1

        def next(self, name):
            assert stream[self.pos] == name, (stream[self.pos], name)
            self.prefetch()
            b = self.pos % NB_W
            u, kind = units[name]
            self.pos += 1
            t = wbuf[b]
            if kind in ("a", "g"):
                v = t[:, :].rearrange("p (k n) -> p k n", k=16)
            elif kind == "b":
                v = t[:, :].rearrange("p (k n) -> p k n", k=8)
            else:
                v = t[:, 0:2048].rearrange("p (k n) -> p k n", k=8)
            return v, ("wbuf", b)

    W = WStream()

    tiles = []
    for i in range(n_pt):
        tiles.append(dict(kind="p", idx=i, T=512, nseg=1, L=512, kv_out=(i == n_pt - 1)))
    tiles.append(dict(kind="s", idx=n_pt, T=256, nseg=4, L=64, kv_out=True))

    def tile_units(t):
        lst = ["q%d" % i for i in range(4)] + ["k%d" % i for i in range(4)]
        lst += ["v00", "v01", "v10", "v11"]
        if t["kv_out"]:
            lst += ["kb00", "kb01", "kb10", "kb11"]
        for i in range(8):
            lst += ["xr%d" % i, "gr%d" % i]
        for i in range(8):
            lst += ["ga%d" % i, "gb%d" % i, "au%d" % i, "ru%d" % i]
        npair = t["T"] // 256
        for pr in range(npair):
            for cg in range(4):
                lst += ["wo%d0" % cg, "wo%d1" % cg]
        lst += ["f1_%d" % i for i in range(32)]
        for pr in range(npair):
            for cg in range(4):
                lst += ["f2_%d%d" % (cg, pc) for pc in range(8)]
        return lst

    for t in tiles:
        stream.extend(tile_units(t))

    S.add("pool", lambda e: e.memset(ident[:], 0.0), writes=[("ident",)])
    S.add("pool", lambda e: e.affine_select(out=ident[:], in_=ident[:], compare_op=ALU.not_equal, fill=1.0,
                                            base=0, pattern=[[-1, 128]], channel_multiplier=1),
          reads=[("ident",)], writes=[("ident",)])
    S.add("dve", lambda e: e.tensor_copy(out=identb[:], in_=ident[:]), reads=[("ident",)], writes=[("identb",)])
    S.add("pool", lambda e: e.memset(VR[0:16, :], 0.0), writes=[VR_b.k()])
    S.add("pool", lambda e: e.memset(SR[0:16, :], 0.0), writes=[SR_b.k()])
    S.add("pool", lambda e: e.memset(hist[:], 0.0), writes=[("hist",)])
    S.add("pool", lambda e: e.memset(hlast[:], 0.0), writes=[("hlast",)])
    S.add("pool", lambda e: e.memset(Vr[:, :, :, 64:65], 1.0), writes=[("Vones",)])

    tile0_units = set(tile_units(tiles[0]))
    conv_order = ["gates"] + [name for name, kind, srcs in unit_src if name != "gates" and name not in tile0_units]
    usrc = {name: (kind, srcs) for name, kind, srcs in unit_src}

    def rec_conv(name):
        u, kind = units[name]
        kind, srcs = usrc[name]
        dst = wsc_view(u, kind)
        if kind == "g":
            S.add("pool", lambda e, dst=dst, s0=srcs[0]: e.dma_start(out=dst[:, :, 0:128], in_=s0),
                  writes=[("wsc_g0",)], dma=True)
            S.add("pool", lambda e, dst=dst, s1=srcs[1]: e.dma_start(out=dst[:, :, 128:256], in_=s1),
                  reads=[("wsc_g0",)], writes=[("wsc", u)], dma=True)
        else:
            S.add("pool", lambda e, dst=dst, s0=srcs[0]: e.dma_start(out=dst, in_=s0),
                  writes=[("wsc", u)], dma=True)

    for name in conv_order:
        rec_conv(name)
    direct_done = set(conv_order)
    u_g = units['gates'][0]
    S.add('sp', lambda e: e.dma_start(out=gbuf[:, :], in_=wsc.ap()[u_g]), reads=[('wsc', u_g)], writes=[('gbuf',)], dma=True)

    S.add("sp", lambda e: e.dma_start(out=VR[0:10, :], in_=vecs_d.ap()), reads=[VR_b.k()], writes=[VR_b.k()], dma=True)
    S.add("sp", lambda e: e.dma_start(out=SR[0:12, :], in_=sconv_d.ap()), reads=[SR_b.k()], writes=[SR_b.k()], dma=True)
    S.add("sp", lambda e: e.dma_start(out=SR[12:16, :], in_=sh_d.ap()), reads=[SR_b.k()], writes=[SR_b.k()], dma=True)

    def tr_rows(src, dstT, key_in, key_out, b):
        def f(e):
            inst = None
            for kc in range(16):
                inst = e.transpose(bank(b)[:, kc * 16:(kc + 1) * 16], src[0:16, kc * 128:(kc + 1) * 128], ident[0:16, 0:16])
            return inst
        S.add("pe", f, reads=list(key_in) + [("ident",)], writes=[pk(b)])
        S.add("dve", lambda e: e.tensor_copy(out=dstT[:, :, :], in_=bank(b)[:, 0:256].rearrange("p (k v) -> p k v", k=16)),
              reads=[pk(b)], writes=[key_out])

    tr_rows(VR, vecT, [VR_b.k()], ("vecT",), 0)
    tr_rows(SR, stT, [SR_b.k()], ("stT",), 1)
    S.add("act", lambda e: e.activation(out=etmp[:, :, 0], in_=vecT[:, :, V_LAM], func=AF.Exp, scale=-1.0),
          reads=[("vecT",)], writes=[("etmp", 0)])
    S.add("act", lambda e: e.activation(out=etmp[:, :, 1], in_=etmp[:, :, 0], func=AF.Ln, bias=1.0),
          reads=[("etmp", 0)], writes=[("etmp", 1)])
    S.add("dve", lambda e: e.tensor_scalar_mul(out=dT[:, :, 2], in0=etmp[:, :, 1], scalar1=-8.0),
          reads=[("etmp", 1)], writes=[("dT", 2)])
    S.add("dve", lambda e: e.tensor_scalar_mul(out=dT[:, :, 3], in0=etmp[:, :, 1], scalar1=-4.0),
          reads=[("etmp", 1)], writes=[("dT", 3)])
    S.add("dve", lambda e: e.tensor_scalar_mul(out=dT[:, :, 0], in0=vecT[:, :, V_BRG], scalar1=0.5),
          reads=[("vecT",)], writes=[("dT", 0)])
    S.add("dve", lambda e: e.tensor_scalar_mul(out=dT[:, :, 1], in0=vecT[:, :, V_BIG], scalar1=0.5),
          reads=[("vecT",)], writes=[("dT", 1)])
    DTK = [("dT", i) for i in range(4)]

    S.add("sp", lambda e: e.dma_start(out=rb_sb[0:16, 0:513], in_=rel_bias_d.ap()), writes=[rbs_b.k()], dma=True)
    S.add("dve", lambda e: e.tensor_copy(out=rb_sb[0:16, 513:1024], in_=rb_sb[0:16, 512:513].broadcast_to([16, 511])),
          reads=[rbs_b.k()], writes=[rbs_b.k()])
    S.add("sp", lambda e: e.dma_start(out=rbx.ap(), in_=rb_sb[0:16, :]), reads=[rbs_b.k()],
          writes=[("rbx",)], dma=True)
    BT = BT_b.ap
    for k in range(128):
        q = "sp" if k % 2 == 0 else "act"
        S.add(q, lambda e, k=k: e.dma_start(out=BT[k:k + 1, :, :], in_=AP(rbx, 256 - k, [[1024 * NH, 1], [1024, NH], [1, 640]])),
              reads=[("rbx",)], writes=[("BTrow", k)], dma=True)
    EBp = EBp_b.ap
    S.add("dve", lambda e: e.tensor_scalar_mul(out=EBp[:, :, :], in0=BT[:, :, :], scalar1=8.0),
          reads=[("BTrow", k) for k in range(128)], writes=EBp_b.keys() + BT_b.keys())
    S.add("dve", lambda e: e.memset(EBp[0:64, :, 4 * 128 + 64:5 * 128], -8000.0), reads=EBp_b.keys(), writes=[("EBm", 0)])
    S.add("dve", lambda e: e.memset(EBp[64:128, :, 0:64], -8000.0), reads=EBp_b.keys(), writes=[("EBm", 1)])
    S.add("sp", lambda e: e.dma_start(out=ebd.ap(), in_=EBp[:, :, :].rearrange("p h n -> p (h n)")),
          reads=[("EBm", 0), ("EBm", 1)] + EBp_b.keys(), writes=[("ebd",)], dma=True)


    evac_toggle = [0]

    def evac_eng():
        evac_toggle[0] ^= 1
        return "act" if evac_toggle[0] else "dve"

    def copy_op(eng, out, in_):
        if eng == "act":
            return lambda e: e.activation(out=out, in_=in_, func=AF.Copy)
        return lambda e: e.tensor_copy(out=out, in_=in_)

    def mm_group_a(wv, wkey, cols, act_ap, act_keys, nk, b, T):
        def f(e):
            inst = None
            for kc in range(nk):
                inst = e.matmul(bank(b)[:, 0:T], lhsT=wv[:, kc, cols], rhs=act_ap[:, kc, 0:T],
                                start=(kc == 0), stop=(kc == nk - 1))
            return inst
        S.add("pe", f, reads=[wkey] + list(act_keys), writes=[pk(b)])

    def norm_phase(t, gcol, alt=False):
        T = t["T"]; ntb = T // 128
        xs_b = xsG_b if alt else xs_b0
        junk_b = junkG_b if alt else junk_b0
        xs = xs_b.ap; junk = junk_b.ap; xnT = xnT_b.ap
        for tb in range(ntb):
            S.add("act", lambda e, tb=tb: e.activation(out=junk[:, :], in_=x_sb[:, tb, :], func=AF.Square,
                                                       accum_out=ss[:, tb:tb + 1]),
                  reads=[("x", tb)], writes=[junk_b.k(), ("ss", tb)])
            S.add("act", lambda e, tb=tb: e.activation(out=sd[:, tb:tb + 1], in_=ss[:, tb:tb + 1], func=AF.Sqrt,
                                                       scale=1.0 / D, bias=EPS),
                  reads=[("ss", tb)], writes=[("sd", tb)])
            S.add("dve", lambda e, tb=tb: e.reciprocal(out=rstd[:, tb:tb + 1], in_=sd[:, tb:tb + 1]),
                  reads=[("sd", tb)], writes=[("rstd", tb)])
            S.add("dve", lambda e, tb=tb: e.tensor_scalar_mul(out=xs[:, tb, :], in0=x_sb[:, tb, :],
                                                              scalar1=rstd[:, tb:tb + 1]),
                  reads=[("x", tb), ("rstd", tb)], writes=[xs_b.k(tb)])
        for kc in range(16):
            b = kc % 8

            def f(e, kc=kc, b=b):
                inst = None
                for tb in range(ntb):
                    inst = e.transpose(bank(b)[:, tb * 128:(tb + 1) * 128], xs[:, tb, kc * 128:(kc + 1) * 128], ident[:, :])
                return inst
            S.add("pe", f, reads=[xs_b.k(tb) for tb in range(ntb)] + [("ident",)], writes=[pk(b)])
            eng = evac_eng()
            if eng == "act":
                fn = lambda e, kc=kc, b=b: e.activation(out=xnT[:, kc, 0:T], in_=bank(b)[:, 0:T], func=AF.Copy,
                                                        scale=vecT[:, kc, gcol:gcol + 1])
            else:
                fn = lambda e, kc=kc, b=b: e.tensor_scalar_mul(out=xnT[:, kc, 0:T], in0=bank(b)[:, 0:T],
                                                               scalar1=vecT[:, kc, gcol:gcol + 1])
            S.add(eng, fn, reads=[pk(b), ("vecT",)], writes=[xnT_b.k(kc)])

    def post_norm(t, pr, gvec_d, store):
        T = t["T"]
        tF = tF_b.ap; gbc = gbc_b.ap
        for tl in range(2):
            tb = pr * 2 + tl
            psrow = PS[tl][:, :]
            pkeys = [pk(tl * 4 + c) for c in range(4)]
            S.add("act", lambda e, psrow=psrow, tl=tl: e.activation(out=tF[:, :], in_=psrow, func=AF.Square,
                                                                    accum_out=ss[:, 4 + tl:5 + tl]),
                  reads=pkeys, writes=[tF_b.k(), ("ss", 4 + tl)])
            S.add("act", lambda e, tl=tl: e.activation(out=sd[:, 4 + tl:5 + tl], in_=ss[:, 4 + tl:5 + tl], func=AF.Sqrt,
                                                       scale=1.0 / D, bias=EPS),
                  reads=[("ss", 4 + tl)], writes=[("sd", 4 + tl)])
            S.add("dve", lambda e, tl=tl: e.reciprocal(out=rstd[:, 4 + tl:5 + tl], in_=sd[:, 4 + tl:5 + tl]),
                  reads=[("sd", 4 + tl)], writes=[("rstd", 4 + tl)])
            S.add("dve", lambda e, psrow=psrow, tl=tl: e.scalar_tensor_tensor(out=tF[:, :], in0=psrow,
                                                                              scalar=rstd[:, 4 + tl:5 + tl], in1=gbc[:, :],
                                                                              op0=ALU.mult, op1=ALU.mult),
                  reads=pkeys + [("rstd", 4 + tl), gbc_b.k()], writes=[tF_b.k()])
            S.add("pool", lambda e, tb=tb: e.tensor_tensor(out=x_sb[:, tb, :], in0=x_sb[:, tb, :], in1=tF[:, :], op=ALU.add),
                  reads=[tF_b.k(), ("x", tb)], writes=[("x", tb)])
            if store is not None:
                dst = store[tb * 128:(tb + 1) * 128, :]
                S.add("pool", lambda e, dst=dst, tb=tb: e.dma_start(out=dst, in_=x_sb[:, tb, :]),
                      reads=[("x", tb)], writes=[("yout", tb)], dma=True)

    def load_gbc(gvec_d):
        gbc = gbc_b.ap
        S.add("sp", lambda e: e.dma_start(out=gbc[:, :], in_=AP(gvec_d, 0, [[0, 128], [1, D]])),
              writes=[gbc_b.k()], dma=True)

    def do_tile(t):
        T = t["T"]; ntb = T // 128; nseg = t["nseg"]; L = t["L"]
        is_p = t["kind"] == "p"
        ti = t["idx"]
        slot = (ti % 2) if is_p else 1
        xnT = xnT_b.ap; qT = qT_b.ap; oT = oT_b.ap; obT = obT_b.ap; mT = mT_b.ap; hid = hid_b.ap
        xsrc = x_p.ap()[ti * 512:(ti + 1) * 512, :] if is_p else x_s.ap()
        ydst = y_p.ap()[ti * 512:(ti + 1) * 512, :] if is_p else y_s.ap()

        for tb in range(ntb):
            S.add("sp", lambda e, xsrc=xsrc, tb=tb: e.dma_start(out=x_sb[:, tb, :], in_=xsrc[tb * 128:(tb + 1) * 128, :]),
                  writes=[("x", tb)], dma=True)
        norm_phase(t, V_PREG)

        if kstop == 'A':
            halted[0] = True
            return
        bctr = 0
        for i in range(4):
            wv, wk = W.next("q%d" % i)
            for j in range(2):
                f = i * 2 + j
                b = bctr % 8; bctr += 1
                mm_group_a(wv, wk, slice(j * 128, (j + 1) * 128), xnT, xnT_b.keys(), 16, b, T)
                eng = evac_eng()
                S.add(eng, copy_op(eng, qT[:, f, 0:T], bank(b)[:, 0:T]), reads=[pk(b)], writes=[qT_b.k(f)])
        if kstop == 'B1':
            halted[0] = True
            return
        for i in range(4):
            wv, wk = W.next("k%d" % i)
            for j in range(2):
                f = i * 2 + j
                b = bctr % 8; bctr += 1
                mm_group_a(wv, wk, slice(j * 128, (j + 1) * 128), xnT, xnT_b.keys(), 16, b, T)
                eng = evac_eng()
                S.add(eng, copy_op(eng, kT[:, f, slot * 512: slot * 512 + T], bank(b)[:, 0:T]),
                      reads=[pk(b)], writes=[("kT", slot, f)])

        if kstop == 'B2':
            halted[0] = True
            return
        def tokmajor_proj(prefix, to_vring, out_d):
            ngrp = 4
            M = 128 if is_p else 64
            for cg in range(2):
                for kh in range(2):
                    wv, wk = W.next("%s%d%d" % (prefix, cg, kh))
                    for g in range(ngrp):
                        b = g + (4 if cg else 0)

                        def f(e, wv=wv, g=g, b=b, kh=kh):
                            inst = None
                            for kc in range(8):
                                inst = e.matmul(bank(b)[0:M, :], lhsT=xnT[:, kh * 8 + kc, g * M:(g + 1) * M],
                                                rhs=wv[:, kc, :], start=(kh == 0 and kc == 0), stop=(kh == 1 and kc == 7))
                            return inst
                        S.add("pe", f, reads=[wk] + xnT_b.keys(), writes=[pk(b)])
                for g in range(ngrp):
                    b = g + (4 if cg else 0)
                    if to_vring:
                        blk = slot * 4 + g
                        eng = evac_eng()
                        dstv = Vr[0:M, blk, cg * 8:(cg + 1) * 8, 0:64]
                        srcv = bank(b)[0:M, :].rearrange("p (h d) -> p h d", h=8)
                        S.add(eng, copy_op(eng, dstv, srcv), reads=[pk(b), ("Vones",)], writes=[("V", blk, cg)])
                    if out_d is not None:
                        st = vst_b.ap
                        sidx = (g + cg) % 2
                        if not to_vring:
                            eng = evac_eng()
                        S.add(eng, copy_op(eng, st[0:M, sidx, :], bank(b)[0:M, :]), reads=[pk(b)], writes=[vst_b.k(sidx)])
                        dst = out_d[g * M:(g + 1) * M, cg * 512:(cg + 1) * 512]
                        kv = os.environ.get('KV', '')
                        if kv == 'nodma':
                            continue
                        srcst = st[0:M, sidx, :] if kv != 'xsrc' else x_sb[0:M, 0, 0:512]
                        S.add(os.environ.get('KQ', 'pool'), lambda e, dst=dst, srcst=srcst: e.dma_start(out=dst, in_=srcst),
                              reads=[vst_b.k(sidx)], writes=[("kvout", prefix, g, cg)], dma=True)

        if t["kv_out"]:
            vout = v_p.ap() if is_p else v_s.ap()
            kout = k_p.ap() if is_p else k_s.ap()
        else:
            vout = kout = None
        tokmajor_proj("v", True, vout if kstop != 'B3n' else None)
        if kstop in ('B3', 'B3n'):
            halted[0] = True
            return
        if t["kv_out"]:
            tokmajor_proj("kb", False, kout)

        if kstop == 'B':
            halted[0] = True
            return
        EB = EB_b.ap
        S.add("sp", lambda e: e.dma_start(out=EB[:, :, :].rearrange("p h n -> p (h n)"), in_=ebd.ap()),
              reads=[("ebd",)], writes=EB_b.keys(), dma=True)
        P = P_b.ap; osb = osb_b.ap
        SB_ = [PS[0][:, 0:1024], PS[0][:, 1024:2048]]
        SBK = [[pk(0), pk(1)], [pk(2), pk(3)]]
        OBS = [bank(4), bank(5)]
        nqb = 4
        for qb in range(nqb):
            if is_p:
                NQ = 128
                qc0 = qb * 128
                G = ti * 4 + qb
                kbl = []
                for jp in range(5):
                    gb = G - jp
                    if gb < 0:
                        continue
                    rs = (gb // 4) % 2
                    kbl.append(dict(jp=jp, nk=128, kcol=rs * 512 + (gb % 4) * 128, vblk=rs * 4 + (gb % 4),
                                    kkey_slot=rs))
            else:
                NQ = 64
                qc0 = qb * 64
                s = qb
                ckst = ckst_b.ap
                for half in range(2):
                    S.add("sp", lambda e, s=s, half=half: e.dma_start(
                        out=ckst[:, :, :], in_=ck_d.ap()[s, half * 256:(half + 1) * 256, :].rearrange("(b p) d -> p b d", p=128)),
                        writes=ckst_b.keys(), dma=True)
                    for c in range(8):
                        b = 6 + (c % 2)

                        def f(e, c=c, b=b):
                            inst = None
                            for bl in range(2):
                                inst = e.transpose(bank(b)[:, bl * 128:(bl + 1) * 128], ckst[:, bl, c * 128:(c + 1) * 128], ident[:, :])
                            return inst
                        S.add("pe", f, reads=ckst_b.keys() + [("ident",)], writes=[pk(b)])
                        eng = evac_eng()
                        S.add(eng, copy_op(eng, kT[:, c, half * 256:(half + 1) * 256], bank(b)[:, 0:256]),
                              reads=[pk(b)], writes=[("kT", 0, c)])
                for blk in range(4):
                    S.add("pool", lambda e, s=s, blk=blk: e.dma_start(
                        out=Vr[:, blk, :, 0:64], in_=cv_d.ap()[s, blk * 128:(blk + 1) * 128, :].rearrange("p (h d) -> p h d", h=NH)),
                        reads=[("Vones",)], writes=[("V", blk, cg) for cg in range(2)], dma=True)
                kbl = [dict(jp=0, nk=64, kcol=512 + s * 64, vblk=4 + s, kkey_slot=1)]
                for jp in range(1, 5):
                    cb = 4 - jp
                    kbl.append(dict(jp=jp, nk=128, kcol=cb * 128, vblk=cb, kkey_slot=0))
            njp = max(kb["jp"] for kb in kbl) + 1
            def attn_S(h):
                hc = h // 2; hp = h % 2
                si = h % 2
                Sps = SB_[si]

                def fS(e, kbl=kbl, hc=hc, hp=hp, Sps=Sps, qc0=qc0, NQ=NQ, h=h):
                    inst = None
                    ncols = njp * 128
                    c = 0
                    while c < ncols:
                        w = min(512, ncols - c)
                        e.matmul(Sps[:, c:c + w], lhsT=identb[:, :], rhs=EB[:, h, c:c + w], start=True, stop=False,
                                 skip_group_check=True)
                        c += w
                    for i, kb in enumerate(kbl):
                        inst = e.matmul(Sps[0:kb["nk"], kb["jp"] * 128: kb["jp"] * 128 + NQ],
                                        lhsT=kT[hp * 64:(hp + 1) * 64, hc, kb["kcol"]: kb["kcol"] + kb["nk"]],
                                        rhs=qT[hp * 64:(hp + 1) * 64, hc, qc0:qc0 + NQ], start=False,
                                        stop=(i == len(kbl) - 1), skip_group_check=True)
                    return inst
                S.add("pe", fS, reads=[("kT", kb["kkey_slot"], hc) for kb in kbl] + [qT_b.k(hc), EB_b.k(h), ("identb",)],
                      writes=SBK[si])
            def attn_rest(h):
                hc = h // 2; hp = h % 2
                si = h % 2
                Sps = SB_[si]
                pi = h % 4
                Sv = Sps[:, 0:njp * 128].rearrange("p (j q) -> p j q", j=njp)[:, :, 0:NQ]
                Pv = P[:, pi, 0:njp * 128].rearrange("p (j q) -> p j q", j=njp)[:, :, 0:NQ]
                S.add("act", lambda e, Sv=Sv, Pv=Pv: e.activation(out=Pv, in_=Sv, func=AF.Exp, scale=0.125),
                      reads=SBK[si], writes=[P_b.k(pi)])
                osl = h % 2
                OB = OBS[osl]

                def fO(e, kbl=kbl, h=h, si=pi, OB=OB, NQ=NQ):
                    inst = None
                    for n, kb in enumerate(kbl):
                        inst = e.matmul(OB[0:NQ, 0:65],
                                        lhsT=P[0:kb["nk"], si, kb["jp"] * 128: kb["jp"] * 128 + NQ],
                                        rhs=Vr[0:kb["nk"], kb["vblk"], h, :], start=(n == 0), stop=(n == len(kbl) - 1))
                    return inst
                S.add("pe", fO, reads=[P_b.k(pi), ("Vones",)] + [("V", kb["vblk"], h // 8) for kb in kbl],
                      writes=[pk(4 + osl)])
                S.add("dve", lambda e, osl=osl, NQ=NQ, OB=OB: e.reciprocal(out=rc[0:NQ, osl:osl + 1], in_=OB[0:NQ, 64:65]),
                      reads=[pk(4 + osl)], writes=[("rc", osl)])
                S.add("dve", lambda e, osl=osl, h=h, NQ=NQ, OB=OB: e.tensor_scalar_mul(out=osb[0:NQ, h * 64:(h + 1) * 64],
                                                                              in0=OB[0:NQ, 0:64],
                                                                              scalar1=rc[0:NQ, osl:osl + 1]),
                      reads=[pk(4 + osl), ("rc", osl)], writes=[("osb", h)])
            attn_S(0)
            for h in range(NH):
                if h + 1 < NH:
                    attn_S(h + 1)
                attn_rest(h)
            for half in range(2):
                b = 6 + half

                def fT(e, half=half, b=b, NQ=NQ):
                    inst = None
                    for c4 in range(4):
                        c = half * 4 + c4
                        inst = e.transpose(bank(b)[:, c4 * 128: c4 * 128 + NQ], osb[0:NQ, c * 128:(c + 1) * 128], ident[0:NQ, 0:NQ])
                    return inst
                S.add("pe", fT, reads=[("osb", hh) for hh in range(NH)] + [("ident",)], writes=[pk(b)])
                eng = evac_eng()
                srcv = bank(b)[:, :].rearrange("p (c q) -> p c q", c=4)[:, :, 0:NQ]
                dstv = oT[:, half * 4:(half + 1) * 4, qc0:qc0 + NQ]
                S.add(eng, copy_op(eng, dstv, srcv), reads=[pk(b)], writes=[oT_b.k(half * 4 + c4) for c4 in range(4)])

        if kstop == 'C':
            halted[0] = True
            return
        gv = gbuf[:, :].rearrange("p (k n) -> p k n", k=16); gk = ("gbuf",)
        xe = xe_b.ap; xc = xc_b.ap; xcb = xcb_b.ap; thr = thr_b.ap; a_ = a_b.ap; a2 = a2_b.ap
        thi = thi_b.ap; u_ = u_b.ap; h_ = h_b.ap; gl = gl_b.ap
        SEGW = 3 + L
        xe3 = xe[:, 0:nseg * SEGW].rearrange("p (s w) -> p s w", s=nseg)

        def v3(ap2d):
            return ap2d.rearrange("p (s w) -> p s w", s=nseg)
        rw = {}
        xc2_b = RB("xc2", R, "R", 8 * KB, [512], F32)
        XC = [xc_b, xc2_b]

        def rnn_mm_xr(n):
            if n % 2 == 0:
                rw["xr"] = W.next("xr%d" % (n // 2))
            j = n % 2
            mm_group_a(rw["xr"][0], rw["xr"][1], slice(j * 128, (j + 1) * 128), xnT, xnT_b.keys(), 16, n % 2, T)

        def rnn_mm_gr(n):
            if n % 2 == 0:
                rw["gr"] = W.next("gr%d" % (n // 2))
            j = n % 2
            mm_group_a(rw["gr"][0], rw["gr"][1], slice(j * 128, (j + 1) * 128), xnT, xnT_b.keys(), 16, 6 + n % 2, T)

        def rnn_s1(n):
            bxr = n % 2
            br, bi = 2 + 2 * (n % 2), 3 + 2 * (n % 2)
            xcn_b = XC[n % 2]
            xcn = xcn_b.ap
            if is_p:
                S.add("pool", lambda e, n=n: e.tensor_copy(out=xe[:, 0:3], in_=hist[:, n, 0:3]),
                      reads=[("hist", n)], writes=[xe_b.k()])
            else:
                S.add("pool", lambda e, n=n: e.tensor_copy(out=xe3[:, :, 0:3],
                                                           in_=stT[:, n, 0:12].rearrange("p (s j) -> p s j", s=4)),
                      reads=[("stT",)], writes=[xe_b.k()])
            S.add("act", lambda e, bxr=bxr: e.activation(out=xe3[:, :, 3:3 + L], in_=v3(bank(bxr)[:, 0:T]), func=AF.Copy),
                  reads=[pk(bxr)], writes=[xe_b.k()])
            S.add("dve", lambda e, bxr=bxr, n=n: e.tensor_scalar(out=xcn[:, 0:T], in0=bank(bxr)[:, 0:T],
                                                                scalar1=vecT[:, n, V_CW0 + 3:V_CW0 + 4], scalar2=vecT[:, n, V_CB:V_CB + 1],
                                                                op0=ALU.mult, op1=ALU.add),
                  reads=[pk(bxr), ("vecT",)], writes=[xcn_b.k()])
            for jt in range(3):
                S.add("dve", lambda e, jt=jt, n=n: e.scalar_tensor_tensor(out=v3(xcn[:, 0:T]), in0=xe3[:, :, jt:jt + L],
                                                                         scalar=vecT[:, n, V_CW0 + jt:V_CW0 + jt + 1],
                                                                         in1=v3(xcn[:, 0:T]), op0=ALU.mult, op1=ALU.add),
                      reads=[xe_b.k(), xcn_b.k(), ("vecT",)], writes=[xcn_b.k()])
            if is_p:
                S.add("pool", lambda e, n=n: e.tensor_copy(out=hist[:, n, 0:3], in_=xe[:, L:L + 3]),
                      reads=[xe_b.k()], writes=[("hist", n)])
                if ti == n_pt - 1:
                    S.add("pool", lambda e, n=n: e.tensor_copy(out=outst[:, n, 0:3], in_=xe[:, L:L + 3]),
                          reads=[xe_b.k()], writes=[("outst", n, 0)])
            else:
                S.add("pool", lambda e, n=n: e.tensor_copy(out=outst[:, n, 0:12].rearrange("p (s j) -> p s j", s=4),
                                                           in_=xe3[:, :, L:L + 3]),
                      reads=[xe_b.k()], writes=[("outst", n, 0)])
            S.add("dve", lambda e: e.tensor_copy(out=xcb[:, 0:T], in_=xcn[:, 0:T]), reads=[xcn_b.k()], writes=[xcb_b.k()])
            S.add("pe", lambda e, n=n, br=br: e.matmul(bank(br)[:, 0:T], lhsT=gv[:, n, 0:128], rhs=xcb[:, 0:T], start=True, stop=True),
                  reads=[gk, xcb_b.k()], writes=[pk(br)])
            S.add("pe", lambda e, n=n, bi=bi: e.matmul(bank(bi)[:, 0:T], lhsT=gv[:, n, 128:256], rhs=xcb[:, 0:T], start=True, stop=True),
                  reads=[gk, xcb_b.k()], writes=[pk(bi)])

        def rnn_s2(n):
            br, bi = 2 + 2 * (n % 2), 3 + 2 * (n % 2)
            bgr = 6 + n % 2
            xcn_b = XC[n % 2]
            xcn = xcn_b.ap
            S.add("act", lambda e, n=n, br=br: e.activation(out=thr[:, 0:T], in_=bank(br)[:, 0:T], func=AF.Tanh, scale=0.5,
                                                           bias=dT[:, n, 0:1]),
                  reads=[pk(br)] + DTK, writes=[thr_b.k()])
            S.add("act", lambda e, n=n: e.activation(out=a_[:, 0:T], in_=thr[:, 0:T], func=AF.Exp, scale=dT[:, n, 3:4], bias=dT[:, n, 3:4]),
                  reads=[thr_b.k()] + DTK, writes=[a_b.k()])
            S.add("dve", lambda e: e.tensor_tensor(out=a2[:, 0:T], in0=a_[:, 0:T], in1=a_[:, 0:T], op=ALU.mult),
                  reads=[a_b.k()], writes=[a2_b.k()])
            S.add("act", lambda e, n=n, bi=bi: e.activation(out=thi[:, 0:T], in_=bank(bi)[:, 0:T], func=AF.Tanh, scale=0.5,
                                                           bias=dT[:, n, 1:2]),
                  reads=[pk(bi)] + DTK, writes=[thi_b.k()])
            S.add("act", lambda e: e.activation(out=a2[:, 0:T], in_=a2[:, 0:T], func=AF.Sqrt, scale=-1.0, bias=1.0),
                  reads=[a2_b.k()], writes=[a2_b.k()])
            S.add("dve", lambda e: e.scalar_tensor_tensor(out=u_[:, 0:T], in0=thi[:, 0:T], scalar=1.0, in1=xcn[:, 0:T],
                                                          op0=ALU.add, op1=ALU.mult),
                  reads=[thi_b.k(), xcn_b.k()], writes=[u_b.k()])
            S.add("dve", lambda e: e.scalar_tensor_tensor(out=u_[:, 0:T], in0=u_[:, 0:T], scalar=0.5, in1=a2[:, 0:T],
                                                          op0=ALU.mult, op1=ALU.mult),
                  reads=[u_b.k(), a2_b.k()], writes=[u_b.k()])
            for sgi in range(nseg):
                if is_p:
                    init = hlast[:, n, 0:1]
                    ikey = ("hlast", n)
                else:
                    init = stT[:, n, 12 + sgi:13 + sgi]
                    ikey = ("stT",)
                S.add("dve", lambda e, sgi=sgi, init=init: e.tensor_tensor_scan(
                    out=h_[:, sgi * L:(sgi + 1) * L], data0=a_[:, sgi * L:(sgi + 1) * L], data1=u_[:, sgi * L:(sgi + 1) * L],
                    initial=init, op0=ALU.mult, op1=ALU.add),
                    reads=[a_b.k(), u_b.k(), ikey], writes=[h_b.k()])
            if is_p:
                S.add("pool", lambda e, n=n: e.tensor_copy(out=hlast[:, n, 0:1], in_=h_[:, L - 1:L]),
                      reads=[h_b.k()], writes=[("hlast", n)])
                if ti == n_pt - 1:
                    S.add("pool", lambda e, n=n: e.tensor_copy(out=outst[:, n, 3:4], in_=h_[:, L - 1:L]),
                          reads=[h_b.k()], writes=[("outst", n, 1)])
            else:
                S.add("pool", lambda e, n=n: e.tensor_copy(out=outst[:, n, 12:16],
                                                           in_=v3(h_[:, 0:T])[:, :, L - 1]),
                      reads=[h_b.k()], writes=[("outst", n, 1)])
            S.add("act", lambda e, bgr=bgr: e.activation(out=gl[:, 0:T], in_=bank(bgr)[:, 0:T], func=AF.Gelu_apprx_tanh),
                  reads=[pk(bgr)], writes=[gl_b.k()])
            S.add("dve", lambda e, n=n: e.tensor_tensor(out=obT[:, n, 0:T], in0=h_[:, 0:T], in1=gl[:, 0:T], op=ALU.mult),
                  reads=[h_b.k(), gl_b.k()], writes=[obT_b.k(n)])

        rnn_mm_xr(0)
        rnn_s1(0)
        for n in range(16):
            if n + 1 < 16:
                rnn_mm_xr(n + 1)
            rnn_mm_gr(n)
            if n + 1 < 16:
                rnn_s1(n + 1)
            rnn_s2(n)

        if (is_p and ti == n_pt - 1) or not is_p:
            ncol = 4 if is_p else 16
            okeys = [("outst", n, c) for n in range(16) for c in range(2)]
            for grp in range(4):
                b = 4 + grp % 2

                def fo(e, grp=grp, b=b, ncol=ncol):
                    inst = None
                    for c4 in range(4):
                        kc = grp * 4 + c4
                        inst = e.transpose(bank(b)[0:ncol, c4 * 128:(c4 + 1) * 128], outst[:, kc, 0:ncol], ident[:, :])
                    return inst
                S.add("pe", fo, reads=okeys + [("ident",)], writes=[pk(b)])
                S.add("dve", lambda e, grp=grp, b=b, ncol=ncol: e.tensor_copy(out=outrow[0:ncol, grp * 512:(grp + 1) * 512],
                                                                            in_=bank(b)[0:ncol, :]),
                      reads=[pk(b)], writes=[outrow_b.k()])
            rk = [outrow_b.k()]
            if is_p:
                S.add("pool", lambda e: e.dma_start(out=conv_p.ap(), in_=outrow[0:3, :]), reads=rk, writes=[("o_conv",)], dma=True)
                S.add("pool", lambda e: e.dma_start(out=h_p.ap(), in_=outrow[3:4, :]), reads=rk, writes=[("o_h",)], dma=True)
            else:
                S.add("pool", lambda e: e.dma_start(out=conv_s.ap(), in_=outrow[0:12, :]), reads=rk, writes=[("o_conv",)], dma=True)
                S.add("pool", lambda e: e.dma_start(out=h_s.ap(), in_=outrow[12:16, :]), reads=rk, writes=[("o_h",)], dma=True)

        if kstop == 'D':
            halted[0] = True
            return
        sg = sg_b.ap; m1 = m1_b.ap
        for i in range(8):
            gav, gak = W.next("ga%d" % i)
            for j in range(2):
                mm_group_a(gav, gak, slice(j * 128, (j + 1) * 128), xnT, xnT_b.keys(), 16, j, T)
                S.add("act", lambda e, j=j: e.activation(out=sg[:, j, 0:T], in_=bank(j)[:, 0:T], func=AF.Sigmoid),
                      reads=[pk(j)], writes=[sg_b.k(j)])
            gbv, gbk = W.next("gb%d" % i)
            for j in range(2):
                mm_group_a(gbv, gbk, slice(j * 128, (j + 1) * 128), xnT, xnT_b.keys(), 16, 2 + j, T)
                S.add("act", lambda e, j=j: e.activation(out=sg[:, 2 + j, 0:T], in_=bank(2 + j)[:, 0:T], func=AF.Sigmoid),
                      reads=[pk(2 + j)], writes=[sg_b.k(2 + j)])
            auv, auk = W.next("au%d" % i)
            for j in range(2):
                mm_group_a(auv, auk, slice(j * 128, (j + 1) * 128), oT, oT_b.keys(), 8, 4 + j, T)
                S.add("dve", lambda e, j=j: e.tensor_tensor(out=m1[:, j, 0:T], in0=sg[:, j, 0:T], in1=bank(4 + j)[:, 0:T], op=ALU.mult),
                      reads=[sg_b.k(j), pk(4 + j)], writes=[m1_b.k(j)])
            ruv, ruk = W.next("ru%d" % i)
            for j in range(2):
                f = i * 2 + j
                mm_group_a(ruv, ruk, slice(j * 128, (j + 1) * 128), obT, obT_b.keys(), 16, 6 + j, T)
                S.add("dve", lambda e, j=j: e.tensor_tensor(out=sg[:, 2 + j, 0:T], in0=sg[:, 2 + j, 0:T], in1=bank(6 + j)[:, 0:T], op=ALU.mult),
                      reads=[sg_b.k(2 + j), pk(6 + j)], writes=[sg_b.k(2 + j)])
                S.add("pool", lambda e, j=j, f=f: e.tensor_tensor(out=mT[:, f, 0:T], in0=m1[:, j, 0:T], in1=sg[:, 2 + j, 0:T], op=ALU.add),
                      reads=[m1_b.k(j), sg_b.k(2 + j)], writes=[mT_b.k(f)])

        if kstop == 'E':
            halted[0] = True
            return
        load_gbc(post_mix_g_d)
        npair = T // 256
        for pr in range(npair):
            for cg in range(4):
                for kh in range(2):
                    wv, wk = W.next("wo%d%d" % (cg, kh))
                    for tl in range(2):
                        tb = pr * 2 + tl
                        b = tl * 4 + cg

                        def f(e, wv=wv, kh=kh, tb=tb, b=b):
                            inst = None
                            for kc in range(8):
                                inst = e.matmul(bank(b)[:, :], lhsT=mT[:, kh * 8 + kc, tb * 128:(tb + 1) * 128], rhs=wv[:, kc, :],
                                                start=(kh == 0 and kc == 0), stop=(kh == 1 and kc == 7))
                            return inst
                        S.add("pe", f, reads=[wk] + mT_b.keys(), writes=[pk(b)])
            post_norm(t, pr, post_mix_g_d, None)

        if kstop == 'F':
            halted[0] = True
            return
        norm_phase(t, V_FFNG, alt=True)

        if kstop == 'G':
            halted[0] = True
            return
        for i in range(32):
            wv, wk = W.next("f1_%d" % i)
            for j in range(2):
                f = i * 2 + j
                b = f % 8
                mm_group_a(wv, wk, slice(j * 128, (j + 1) * 128), xnT, xnT_b.keys(), 16, b, T)
                ri = f % 2
                S.add("act", lambda e, b=b, ri=ri: e.activation(out=rl[:, ri, 0:T], in_=bank(b)[:, 0:T], func=AF.Relu),
                      reads=[pk(b)], writes=[("rl", ri)])
                S.add("dve", lambda e, b=b, ri=ri, f=f: e.tensor_tensor(out=hid[:, f, 0:T], in0=bank(b)[:, 0:T], in1=rl[:, ri, 0:T], op=ALU.mult),
                      reads=[pk(b), ("rl", ri)], writes=[hid_b.k(f)])

        if kstop == 'H':
            halted[0] = True
            return
        load_gbc(post_ffn_g_d)
        for pr in range(npair):
            for cg in range(4):
                for pc in range(8):
                    wv, wk = W.next("f2_%d%d" % (cg, pc))
                    for tl in range(2):
                        tb = pr * 2 + tl
                        b = tl * 4 + cg

                        def f(e, wv=wv, pc=pc, tb=tb, b=b):
                            inst = None
                            for kc in range(8):
                                inst = e.matmul(bank(b)[:, :], lhsT=hid[:, pc * 8 + kc, tb * 128:(tb + 1) * 128], rhs=wv[:, kc, :],
                                                start=(pc == 0 and kc == 0), stop=(pc == 7 and kc == 7))
                            return inst
                        S.add("pe", f, reads=[wk] + hid_b.keys(), writes=[pk(b)])
            post_norm(t, pr, post_ffn_g_d, ydst)

    for t in tiles:
        if kstop == 'pro' or halted[0]:
            break
        do_tile(t)

    assert kstop or W.pos == len(stream)
    S.emit(nc, es)
    es.close()
    return nc


_NC_CACHE = {}


def _get_nc(n_pt):
    if n_pt not in _NC_CACHE:
        _NC_CACHE[n_pt] = build_program(n_pt)
    return _NC_CACHE[n_pt]


def kernel(x_prompt, x_sample, cache_k, cache_v, state_conv, state_h,
           pre_mix_g, w_in, rel_bias, conv_w, conv_b, w_rg, b_rg, w_ig, b_ig, lru_lambda,
           w_attn_up, w_rnn_up, w_out, post_mix_g, pre_ffn_g, w_ff1, w_ff2, post_ffn_g):
    f = lambda a: np.ascontiguousarray(np.asarray(a, dtype=np.float32))
    x_prompt = f(x_prompt); x_sample = f(x_sample)
    B, SEQ, _ = x_prompt.shape
    n_pt = SEQ // 512
    ncores = 8
    assert B == ncores and x_sample.shape[0] == 4 * ncores
    vecs = np.concatenate([f(pre_mix_g)[0][None], f(pre_ffn_g)[0][None], f(conv_w)[0], f(conv_b)[0][None],
                           f(b_rg)[0][None], f(b_ig)[0][None], f(lru_lambda)[0][None]], axis=0)
    vecs = np.ascontiguousarray(vecs)
    shared = {
        "vecs": vecs, "post_mix_g": f(post_mix_g)[0][None].copy(), "post_ffn_g": f(post_ffn_g)[0][None].copy(),
        "w_in": f(w_in)[0], "rel_bias": f(rel_bias)[0], "w_rg": f(w_rg)[0], "w_ig": f(w_ig)[0],
        "w_attn_up": f(w_attn_up)[0], "w_rnn_up": f(w_rnn_up)[0], "w_out": f(w_out)[0],
        "w_ff1": f(w_ff1)[0], "w_ff2": f(w_ff2)[0],
    }
    ck = f(cache_k)[0].reshape(32, 512, DA); cv = f(cache_v)[0].reshape(32, 512, DA)
    sc = f(state_conv)[0]; sh = f(state_h)[0]
    in_maps = []
    for c in range(ncores):
        m = dict(shared)
        m["x_p"] = x_prompt[c]
        m["x_s"] = np.ascontiguousarray(x_sample[4 * c:4 * c + 4].reshape(256, D))
        m["ck"] = np.ascontiguousarray(ck[4 * c:4 * c + 4])
        m["cv"] = np.ascontiguousarray(cv[4 * c:4 * c + 4])
        m["sconv"] = np.ascontiguousarray(sc[4 * c:4 * c + 4].reshape(12, D))
        m["sh"] = np.ascontiguousarray(sh[4 * c:4 * c + 4])
        in_maps.append(m)
    nc = _get_nc(n_pt)
    res = run_bass_kernel_spmd(nc, in_maps, core_ids=list(range(ncores)))
    r = res.results
    y_prompt = np.stack([r[c]["y_p"] for c in range(ncores)], 0)
    y_sample = np.concatenate([r[c]["y_s"].reshape(4, 64, D) for c in range(ncores)], 0)
    k_prompt = np.stack([r[c]["k_p"].reshape(512, NH, DH) for c in range(ncores)], 0)[None]
    v_prompt = np.stack([r[c]["v_p"].reshape(512, NH, DH) for c in range(ncores)], 0)[None]
    conv_prompt = np.stack([r[c]["conv_p"] for c in range(ncores)], 0)[None]
    h_prompt = np.stack([r[c]["h_p"][0] for c in range(ncores)], 0)[None]
    k_sample = np.concatenate([r[c]["k_s"].reshape(4, 64, NH, DH) for c in range(ncores)], 0)[None]
    v_sample = np.concatenate([r[c]["v_s"].reshape(4, 64, NH, DH) for c in range(ncores)], 0)[None]
    conv_sample = np.concatenate([r[c]["conv_s"].reshape(4, 3, D) for c in range(ncores)], 0)[None]
    h_sample = np.concatenate([r[c]["h_s"] for c in range(ncores)], 0)[None]
    outs = (y_prompt, y_sample, k_prompt, v_prompt, conv_prompt, h_prompt,
            k_sample, v_sample, conv_sample, h_sample)
    return tuple(np.ascontiguousarray(o, dtype=np.float32) for o in outs)
```

```python
import numpy as np
from contextlib import ExitStack
import concourse.bass as bass
import concourse.mybir as mybir
from concourse.ap import AP
from concourse.bass_utils import run_bass_kernel_spmd

F32 = mybir.dt.float32
BF16 = mybir.dt.bfloat16
AF = mybir.ActivationFunctionType
ALU = mybir.AluOpType

D = 2048
KC = 16
NH = 16
DH = 64
DA = 1024
DFF = 8192
NIN = 11264
EPS = 1e-6
C_Q, C_K, C_V, C_XR, C_GR, C_GA, C_GB = 0, 1024, 2048, 3072, 5120, 7168, 9216
NB_W = 5
K_DMA = 8


class Op:
    __slots__ = ("eng", "fn", "reads", "writes", "dma", "deps", "sem", "val", "need")

    def __init__(self, eng, fn, reads, writes, dma):
        self.eng = eng
        self.fn = fn
        self.reads = reads
        self.writes = writes
        self.dma = dma
        self.deps = ()
        self.sem = None
        self.val = 0
        self.need = False


class Sched:
    def __init__(self):
        self.ops = []
        self.ranges = {}

    def add(self, eng, fn, reads=(), writes=(), dma=False):
        self.ops.append(Op(eng, fn, tuple(reads), tuple(writes), dma))

    def set_range(self, key, region, lo, hi):
        self.ranges[key] = (region, lo, hi)

    def analyse(self):
        last_w = {}
        readers = {}
        pages = {}
        ops = self.ops

        def live_add(k):
            r = self.ranges.get(k)
            if r is None:
                return
            reg, lo, hi = r
            for p in range(lo // 1024, (hi + 1023) // 1024):
                pages.setdefault((reg, p), set()).add(k)

        def overlapping(k):
            r = self.ranges.get(k)
            if r is None:
                return ()
            reg, lo, hi = r
            out = set()
            for p in range(lo // 1024, (hi + 1023) // 1024):
                for k2 in pages.get((reg, p), ()):
                    if k2 != k:
                        r2 = self.ranges[k2]
                        if r2[1] < hi and lo < r2[2]:
                            out.add(k2)
            return out

        def want(i, j, raw):
            a, b = ops[j], ops[i]
            if a.dma or b.dma:
                return True
            if a.eng != b.eng:
                return True
            if a.eng == "pe":
                return False
            return raw

        for i, op in enumerate(ops):
            deps = set()
            for k in op.reads:
                j = last_w.get(k)
                if j is not None and want(i, j, True):
                    deps.add(j)
                if k[0] == "ps":
                    for j in readers.get(k, ()):
                        if ops[j].eng != op.eng:
                            deps.add(j)
            for k in op.writes:
                j = last_w.get(k)
                if j is not None and want(i, j, False):
                    deps.add(j)
                for j in readers.get(k, ()):
                    if j != i and want(i, j, False):
                        deps.add(j)
                for k2 in overlapping(k):
                    j = last_w.get(k2)
                    if j is not None and want(i, j, False):
                        deps.add(j)
                    for j in readers.get(k2, ()):
                        if want(i, j, False):
                            deps.add(j)
            for k in op.writes:
                last_w[k] = i
                readers[k] = []
                live_add(k)
            for k in op.reads:
                readers.setdefault(k, []).append(i)
                live_add(k)
            deps.discard(i)
            op.deps = tuple(sorted(deps))
            for j in op.deps:
                ops[j].need = True

    def emit(self, nc, es):
        self.analyse()
        ops = self.ops
        engs = ["pe", "act", "dve", "pool", "sp"]
        esem = {e: es.enter_context(nc.semaphore("sem_" + e)) for e in engs if e != "sp"}
        dq = {q: [es.enter_context(nc.semaphore("dq_%s_%d" % (q, i))) for i in range(K_DMA)]
              for q in ("sp", "pool", "act")}
        cnt = {e: 0 for e in engs}
        dcnt = {q: 0 for q in dq}
        prev_use = {}
        for i, op in enumerate(ops):
            if op.dma:
                j = dcnt[op.eng]
                dcnt[op.eng] += 1
                op.sem = dq[op.eng][j % K_DMA]
                op.val = 16 * (j // K_DMA + 1)
                prev_use[i] = (op.sem, op.val - 16) if j >= K_DMA else None
            elif op.need:
                cnt[op.eng] += 1
                op.sem = esem[op.eng]
                op.val = cnt[op.eng]
        final = {q: [(dq[q][s], 16 * ((dcnt[q] - 1 - s) // K_DMA + 1)) for s in range(K_DMA) if dcnt[q] > s]
                 for q in dq}
        block = es.enter_context(nc.Block())

        def run(engname):
            def body(e):
                waited = {}

                def wait(sem, val):
                    if val <= 0:
                        return
                    key = id(sem)
                    if waited.get(key, 0) >= val:
                        return
                    waited[key] = val
                    e.wait_ge(sem, val)

                for i, op in enumerate(ops):
                    if op.eng != engname:
                        continue
                    for j in op.deps:
                        p = ops[j]
                        wait(p.sem, p.val)
                    if op.dma and prev_use[i] is not None:
                        wait(*prev_use[i])
                    inst = op.fn(e)
                    if op.dma:
                        inst.then_inc(op.sem, 16)
                    elif op.need:
                        inst.then_inc(op.sem, 1)
                for (sem, val) in final.get(engname, ()):
                    wait(sem, val)
            return body

        block.tensor(run("pe"))
        block.scalar(run("act"))
        block.vector(run("dve"))
        block.gpsimd(run("pool"))
        block.sync(run("sp"))


def build_program(n_pt=8, kstop=None):
    import os
    kstop = kstop or os.environ.get('KSTOP')
    halted = [False]
    nc = bass.Bass("TRN2", target_bir_lowering=False)
    es = ExitStack()
    S = Sched()
    SP_LEN = n_pt * 512

    def din(name, shape):
        return nc.dram_tensor(name, shape, F32, kind="ExternalInput")

    def dout(name, shape):
        return nc.dram_tensor(name, shape, F32, kind="ExternalOutput")

    x_p = din("x_p", [SP_LEN, D]); x_s = din("x_s", [256, D])
    ck_d = din("ck", [4, 512, DA]); cv_d = din("cv", [4, 512, DA])
    sconv_d = din("sconv", [12, D]); sh_d = din("sh", [4, D])
    vec_names = ["pre_mix_g", "pre_ffn_g", "conv_w0", "conv_w1", "conv_w2", "conv_w3",
                 "conv_b", "b_rg", "b_ig", "lru_lambda"]
    V_PREG, V_FFNG, V_CW0, V_CB, V_BRG, V_BIG, V_LAM = 0, 1, 2, 6, 7, 8, 9
    vecs_d = din("vecs", [10, D])
    post_mix_g_d = din("post_mix_g", [1, D]); post_ffn_g_d = din("post_ffn_g", [1, D])
    w_in_d = din("w_in", [D, NIN]); rel_bias_d = din("rel_bias", [NH, 513])
    w_rg_d = din("w_rg", [16, 128, 128]); w_ig_d = din("w_ig", [16, 128, 128])
    w_au_d = din("w_attn_up", [DA, D]); w_ru_d = din("w_rnn_up", [D, D]); w_out_d = din("w_out", [D, D])
    w_f1_d = din("w_ff1", [D, DFF]); w_f2_d = din("w_ff2", [DFF, D])

    y_p = dout("y_p", [SP_LEN, D]); y_s = dout("y_s", [256, D])
    k_p = dout("k_p", [512, DA]); v_p = dout("v_p", [512, DA])
    conv_p = dout("conv_p", [3, D]); h_p = dout("h_p", [1, D])
    k_s = dout("k_s", [256, DA]); v_s = dout("v_s", [256, DA])
    conv_s = dout("conv_s", [12, D]); h_s = dout("h_s", [4, D])

    units = {}
    unit_src = []

    def add_unit(name, kind, srcs):
        units[name] = (len(unit_src), kind)
        unit_src.append((name, kind, srcs))

    def wslice(w, k0, nk, c0, ncol):
        return w.ap()[k0 * 128:(k0 + nk) * 128, c0:c0 + ncol].rearrange("(kc p) n -> p kc n", p=128)

    for i in range(4):
        add_unit("q%d" % i, "a", [wslice(w_in_d, 0, 16, C_Q + i * 256, 256)])
    for i in range(4):
        add_unit("k%d" % i, "a", [wslice(w_in_d, 0, 16, C_K + i * 256, 256)])
    for cg in range(2):
        for kh in range(2):
            add_unit("v%d%d" % (cg, kh), "b", [wslice(w_in_d, kh * 8, 8, C_V + cg * 512, 512)])
    for cg in range(2):
        for kh in range(2):
            add_unit("kb%d%d" % (cg, kh), "b", [wslice(w_in_d, kh * 8, 8, C_K + cg * 512, 512)])
    add_unit("gates", "g", [w_rg_d.ap().rearrange("n c d -> c n d"), w_ig_d.ap().rearrange("n c d -> c n d")])
    for i in range(8):
        add_unit("xr%d" % i, "a", [wslice(w_in_d, 0, 16, C_XR + i * 256, 256)])
        add_unit("gr%d" % i, "a", [wslice(w_in_d, 0, 16, C_GR + i * 256, 256)])
    for i in range(8):
        add_unit("ga%d" % i, "a", [wslice(w_in_d, 0, 16, C_GA + i * 256, 256)])
        add_unit("gb%d" % i, "a", [wslice(w_in_d, 0, 16, C_GB + i * 256, 256)])
        add_unit("au%d" % i, "h", [wslice(w_au_d, 0, 8, i * 256, 256)])
        add_unit("ru%d" % i, "a", [wslice(w_ru_d, 0, 16, i * 256, 256)])
    for cg in range(4):
        for kh in range(2):
            add_unit("wo%d%d" % (cg, kh), "b", [wslice(w_out_d, kh * 8, 8, cg * 512, 512)])
    for i in range(32):
        add_unit("f1_%d" % i, "a", [wslice(w_f1_d, 0, 16, i * 256, 256)])
    for cg in range(4):
        for pc in range(8):
            add_unit("f2_%d%d" % (cg, pc), "b", [wslice(w_f2_d, pc * 8, 8, cg * 512, 512)])
    NU = len(unit_src)
    wsc = nc.dram_tensor("wsc", [NU, 128, 4096], BF16, kind="Internal")
    rbx = nc.dram_tensor("rbx", [NH, 1024], F32, kind="Internal")
    btd = nc.dram_tensor("btd", [128, NH * 640], F32, kind="Internal")
    ebd = nc.dram_tensor("ebd", [128, NH * 640], BF16, kind="Internal")

    def wsc_view(u, kind):
        base = wsc.ap()[u]
        if kind == "a":
            return base.rearrange("p (k n) -> p k n", k=16)
        if kind == "b":
            return base.rearrange("p (k n) -> p k n", k=8)
        if kind == "h":
            return base[:, 0:2048].rearrange("p (k n) -> p k n", k=8)
        if kind == "g":
            return base.rearrange("p (k n) -> p k n", k=16)
        raise ValueError(kind)

    def sb(name, shape, dt):
        return es.enter_context(nc.sbuf_tensor(name, shape, dt))

    R = sb("R", [128, 32768], BF16)
    R2 = sb("R2", [128, 8192], BF16)
    x_sb = sb("x_sb", [128, 4, D], F32)
    kT = sb("kT", [128, 8, 1024], BF16)
    Vr = sb("Vr", [128, 8, NH, 65], BF16)
    wbuf = [sb("wbuf%d" % i, [128, 4096], BF16) for i in range(NB_W)]
    rl = sb("rl", [128, 2, 512], F32)
    gbuf = sb("gbuf", [128, 4096], BF16)
    ident = sb("ident", [128, 128], F32)
    identb = sb("identb", [128, 128], BF16)
    vecT = sb("vecT", [128, 16, 16], F32)
    stT = sb("stT", [128, 16, 16], F32)
    dT = sb("dT", [128, 16, 8], F32)
    etmp = sb("etmp", [128, 16, 2], F32)
    hist = sb("hist", [128, 16, 4], F32)
    hlast = sb("hlast", [128, 16, 4], F32)
    outst = sb("outst", [128, 16, 16], F32)
    ss = sb("ss", [128, 8], F32)
    sd = sb("sd", [128, 8], F32)
    rstd = sb("rstd", [128, 8], F32)
    rc = sb("rc", [128, 4], F32)
    PS = [es.enter_context(nc.psum_tensor("PS%d" % i, [128, 2048], F32)) for i in range(2)]

    def bank(b):
        return PS[b // 4][:, (b % 4) * 512:(b % 4 + 1) * 512]

    def pk(b):
        return ("ps", b)

    class RB:
        def __init__(self, name, region, reg_name, off, shape, dt):
            self.name = name
            esz = 4 if dt == F32 else 2
            n = int(np.prod(shape))
            self.off = off
            self.nbytes = n * esz
            base = region[:, off // 2: off // 2 + n * esz // 2]
            if dt == F32:
                base = base.bitcast(F32)
            if len(shape) == 1:
                self.ap = base
            elif len(shape) == 2:
                self.ap = base.rearrange("p (a b) -> p a b", a=shape[0])
            else:
                self.ap = base.rearrange("p (a b c) -> p a b c", a=shape[0], b=shape[1])
            self.sub = self.nbytes // shape[0] if len(shape) >= 2 else self.nbytes
            self.n0 = shape[0] if len(shape) >= 2 else 1
            self.reg_name = reg_name
            for i in range(self.n0):
                S.set_range((name, i), reg_name, off + i * self.sub, off + (i + 1) * self.sub)

        def k(self, i=0):
            return (self.name, i)

        def keys(self):
            return [(self.name, i) for i in range(self.n0)]

    KB = 1024
    xs_b = RB("xs", R, "R", 0, [4, D], F32)
    junk_b = RB("junk", R, "R", 32 * KB, [D], F32)
    xs_b0 = xs_b; junk_b0 = junk_b
    xsG_b = RB("xsG", R, "R", 16 * KB, [4, D], F32)
    junkG_b = RB("junkG", R, "R", 48 * KB, [D], F32)
    qT_b = RB("qT", R, "R", 0, [8, 512], BF16)
    P_b = RB("P", R, "R", 8 * KB, [4, 640], BF16)
    osb_b = RB("osb", R, "R", 16 * KB, [1024], F32)
    oT_b = RB("oT", R, "R", 20 * KB, [8, 512], BF16)
    EB_b = RB("EB", R, "R", 28 * KB, [NH, 640], BF16)
    vst_b = RB("vst", R, "R", 48 * KB, [2, 512], F32)
    ckst_b = RB("ckst", R, "R", 52 * KB, [2, 1024], F32)
    obT_b = RB("obT", R, "R", 28 * KB, [16, 512], BF16)
    t0 = 44 * KB
    xe_b = RB("xe", R, "R", t0, [520], F32)
    xc_b = RB("xc", R, "R", t0 + 2080, [512], F32)
    xcb_b = RB("xcb", R, "R", t0 + 4128, [512], BF16)
    thr_b = RB("thr", R, "R", t0 + 5152, [512], F32)
    a_b = RB("a", R, "R", t0 + 7200, [512], F32)
    a2_b = RB("a2", R, "R", t0 + 9248, [512], F32)
    thi_b = RB("thi", R, "R", t0 + 11296, [512], F32)
    u_b = RB("u", R, "R", t0 + 13344, [512], F32)
    h_b = RB("h", R, "R", t0 + 15392, [512], F32)
    gl_b = RB("gl", R, "R", t0 + 17440, [512], F32)
    sg_b = RB("sg", R, "R", 44 * KB, [4, 512], F32)
    m1_b = RB("m1", R, "R", 52 * KB, [2, 512], F32)
    mT_b = RB("mT", R, "R", 0, [16, 512], BF16)
    hid_b = RB("hid", R, "R", 0, [64, 512], BF16)
    BT_b = RB("BT", R, "R", 0, [NH, 640], F32)
    EBp_b = RB("EBp", R, "R", 40 * KB, [NH, 640], BF16)
    xnT_b = RB("xnT", R2, "R2", 0, [16, 512], BF16)
    VR_b = RB("VRb", R2, "R2", 0, [D], F32)
    SR_b = RB("SRb", R2, "R2", 8 * KB, [D], F32)
    rbs_b = RB("rbs", R, "R", 60 * KB, [1024], F32)
    outrow_b = RB("outrow", R, "R", 0, [D], F32)
    VR = VR_b.ap; SR = SR_b.ap; rb_sb = rbs_b.ap; outrow = outrow_b.ap
    tF_b = RB("tF", R2, "R2", 0, [D], F32)
    gbc_b = RB("gbc", R2, "R2", 8 * KB, [D], F32)

    stream = []

    class WStream:
        def __init__(self):
            self.pos = 0
            self.loaded = 0

        def _record_load(self, j):
            name = stream[j]
            u, kind = units[name]
            b = j % NB_W
            n = 2048 if kind == "h" else 4096
            if name not in direct_done:
                direct_done.add(name)
                kk = 16 if kind == "a" else 8
                dstv = wbuf[b][:, 0:n].rearrange("p (k n) -> p k n", k=kk)
                src = usrc[name][1][0]
                S.add("pool", lambda e, dstv=dstv, src=src: e.dma_start(out=dstv, in_=src),
                      writes=[("wbuf", b)], dma=True)
                S.add("sp", lambda e, u=u, b=b, n=n: e.dma_start(out=wsc.ap()[u][:, 0:n], in_=wbuf[b][:, 0:n]),
                      reads=[("wbuf", b)], writes=[("wsc", u)], dma=True)
                return
            S.add("sp", lambda e, u=u, b=b, n=n: e.dma_start(out=wbuf[b][:, 0:n], in_=wsc.ap()[u][:, 0:n]),
                  reads=[("wsc", u)], writes=[("wbuf", b)], dma=True)

        def prefetch(self):
            while self.loaded < len(stream) and self.loaded < self.pos + NB_W - 1:
                self._record_load(self.loaded)
                self.loaded += 1

        def next(self, name):
            assert stream[self.pos] == name, (stream[self.pos], name)
            self.prefetch()
            b = self.pos % NB_W
            u, kind = units[name]
            self.pos += 1
            t = wbuf[b]
            if kind in ("a", "g"):
                v = t[:, :].rearrange("p (k n) -> p k n", k=16)
            elif kind == "b":
                v = t[:, :].rearrange("p (k n) -> p k n", k=8)
            else:
                v = t[:, 0:2048].rearrange("p (k n) -> p k n", k=8)
            return v, ("wbuf", b)

    W = WStream()

    tiles = []
    for i in range(n_pt):
        tiles.append(dict(kind="p", idx=i, T=512, nseg=1, L=512, kv_out=(i == n_pt - 1)))
    tiles.append(dict(kind="s", idx=n_pt, T=256, nseg=4, L=64, kv_out=True))

    def tile_units(t):
        lst = ["q%d" % i for i in range(4)] + ["k%d" % i for i in range(4)]
        lst += ["v00", "v01", "v10", "v11"]
        if t["kv_out"]:
            lst += ["kb00", "kb01", "kb10", "kb11"]
        for i in range(8):
            lst += ["xr%d" % i, "gr%d" % i]
        for i in range(8):
            lst += ["ga%d" % i, "gb%d" % i, "au%d" % i, "ru%d" % i]
        npair = t["T"] // 256
        for pr in range(npair):
            for cg in range(4):
                lst += ["wo%d0" % cg, "wo%d1" % cg]
        lst += ["f1_%d" % i for i in range(32)]
        for pr in range(npair):
            for cg in range(4):
                lst += ["f2_%d%d" % (cg, pc) for pc in range(8)]
        return lst

    for t in tiles:
        stream.extend(tile_units(t))

    S.add("pool", lambda e: e.memset(ident[:], 0.0), writes=[("ident",)])
    S.add("pool", lambda e: e.affine_select(out=ident[:], in_=ident[:], compare_op=ALU.not_equal, fill=1.0,
                                            base=0, pattern=[[-1, 128]], channel_multiplier=1),
          reads=[("ident",)], writes=[("ident",)])
    S.add("dve", lambda e: e.tensor_copy(out=identb[:], in_=ident[:]), reads=[("ident",)], writes=[("identb",)])
    S.add("pool", lambda e: e.memset(VR[0:16, :], 0.0), writes=[VR_b.k()])
    S.add("pool", lambda e: e.memset(SR[0:16, :], 0.0), writes=[SR_b.k()])
    S.add("pool", lambda e: e.memset(hist[:], 0.0), writes=[("hist",)])
    S.add("pool", lambda e: e.memset(hlast[:], 0.0), writes=[("hlast",)])
    S.add("pool", lambda e: e.memset(Vr[:, :, :, 64:65], 1.0), writes=[("Vones",)])

    tile0_units = set(tile_units(tiles[0]))
    conv_order = ["gates"] + [name for name, kind, srcs in unit_src if name != "gates" and name not in tile0_units]
    usrc = {name: (kind, srcs) for name, kind, srcs in unit_src}

    def rec_conv(name):
        u, kind = units[name]
        kind, srcs = usrc[name]
        dst = wsc_view(u, kind)
        if kind == "g":
            S.add("pool", lambda e, dst=dst, s0=srcs[0]: e.dma_start(out=dst[:, :, 0:128], in_=s0),
                  writes=[("wsc_g0",)], dma=True)
            S.add("pool", lambda e, dst=dst, s1=srcs[1]: e.dma_start(out=dst[:, :, 128:256], in_=s1),
                  reads=[("wsc_g0",)], writes=[("wsc", u)], dma=True)
        else:
            S.add("pool", lambda e, dst=dst, s0=srcs[0]: e.dma_start(out=dst, in_=s0),
                  writes=[("wsc", u)], dma=True)

    for name in conv_order:
        rec_conv(name)
    direct_done = set(conv_order)
    u_g = units['gates'][0]
    S.add('sp', lambda e: e.dma_start(out=gbuf[:, :], in_=wsc.ap()[u_g]), reads=[('wsc', u_g)], writes=[('gbuf',)], dma=True)

    S.add("sp", lambda e: e.dma_start(out=VR[0:10, :], in_=vecs_d.ap()), reads=[VR_b.k()], writes=[VR_b.k()], dma=True)
    S.add("sp", lambda e: e.dma_start(out=SR[0:12, :], in_=sconv_d.ap()), reads=[SR_b.k()], writes=[SR_b.k()], dma=True)
    S.add("sp", lambda e: e.dma_start(out=SR[12:16, :], in_=sh_d.ap()), reads=[SR_b.k()], writes=[SR_b.k()], dma=True)

    def tr_rows(src, dstT, key_in, key_out, b):
        def f(e):
            inst = None
            for kc in range(16):
                inst = e.transpose(bank(b)[:, kc * 16:(kc + 1) * 16], src[0:16, kc * 128:(kc + 1) * 128], ident[0:16, 0:16])
            return inst
        S.add("pe", f, reads=list(key_in) + [("ident",)], writes=[pk(b)])
        S.add("dve", lambda e: e.tensor_copy(out=dstT[:, :, :], in_=bank(b)[:, 0:256].rearrange("p (k v) -> p k v", k=16)),
              reads=[pk(b)], writes=[key_out])

    tr_rows(VR, vecT, [VR_b.k()], ("vecT",), 0)
    tr_rows(SR, stT, [SR_b.k()], ("stT",), 1)
    S.add("act", lambda e: e.activation(out=etmp[:, :, 0], in_=vecT[:, :, V_LAM], func=AF.Exp, scale=-1.0),
          reads=[("vecT",)], writes=[("etmp", 0)])
    S.add("act", lambda e: e.activation(out=etmp[:, :, 1], in_=etmp[:, :, 0], func=AF.Ln, bias=1.0),
          reads=[("etmp", 0)], writes=[("etmp", 1)])
    S.add("dve", lambda e: e.tensor_scalar_mul(out=dT[:, :, 2], in0=etmp[:, :, 1], scalar1=-8.0),
          reads=[("etmp", 1)], writes=[("dT", 2)])
    S.add("dve", lambda e: e.tensor_scalar_mul(out=dT[:, :, 3], in0=etmp[:, :, 1], scalar1=-4.0),
          reads=[("etmp", 1)], writes=[("dT", 3)])
    S.add("dve", lambda e: e.tensor_scalar_mul(out=dT[:, :, 0], in0=vecT[:, :, V_BRG], scalar1=0.5),
          reads=[("vecT",)], writes=[("dT", 0)])
    S.add("dve", lambda e: e.tensor_scalar_mul(out=dT[:, :, 1], in0=vecT[:, :, V_BIG], scalar1=0.5),
          reads=[("vecT",)], writes=[("dT", 1)])
    DTK = [("dT", i) for i in range(4)]

    S.add("sp", lambda e: e.dma_start(out=rb_sb[0:16, 0:513], in_=rel_bias_d.ap()), writes=[rbs_b.k()], dma=True)
    S.add("dve", lambda e: e.tensor_copy(out=rb_sb[0:16, 513:1024], in_=rb_sb[0:16, 512:513].broadcast_to([16, 511])),
          reads=[rbs_b.k()], writes=[rbs_b.k()])
    S.add("sp", lambda e: e.dma_start(out=rbx.ap(), in_=rb_sb[0:16, :]), reads=[rbs_b.k()],
          writes=[("rbx",)], dma=True)
    BT = BT_b.ap
    for k in range(128):
        q = "sp" if k % 2 == 0 else "act"
        S.add(q, lambda e, k=k: e.dma_start(out=btd.ap()[k:k + 1, :].rearrange("o (h n) -> o h n", h=NH),
                                            in_=AP(rbx, 256 - k, [[1024 * NH, 1], [1024, NH], [1, 640]])),
              reads=[("rbx",)], writes=[("BTrow", k)], dma=True)
    S.add("sp", lambda e: e.dma_start(out=BT[:, :, :].rearrange("p h n -> p (h n)"), in_=btd.ap()),
          reads=[("BTrow", k) for k in range(128)], writes=[("BTall",)] + BT_b.keys(), dma=True)
    EBp = EBp_b.ap
    S.add("dve", lambda e: e.tensor_scalar_mul(out=EBp[:, :, :], in0=BT[:, :, :], scalar1=8.0),
          reads=[("BTall",)] + BT_b.keys(), writes=EBp_b.keys())
    S.add("dve", lambda e: e.memset(EBp[0:64, :, 4 * 128 + 64:5 * 128], -8000.0), reads=EBp_b.keys(), writes=[("EBm", 0)])
    S.add("dve", lambda e: e.memset(EBp[64:128, :, 0:64], -8000.0), reads=EBp_b.keys(), writes=[("EBm", 1)])
    S.add("sp", lambda e: e.dma_start(out=ebd.ap(), in_=EBp[:, :, :].rearrange("p h n -> p (h n)")),
          reads=[("EBm", 0), ("EBm", 1)] + EBp_b.keys(), writes=[("ebd",)], dma=True)


    evac_toggle = [0]

    def evac_eng():
        evac_toggle[0] ^= 1
        return "act" if evac_toggle[0] else "dve"

    def copy_op(eng, out, in_):
        if eng == "act":
            return lambda e: e.activation(out=out, in_=in_, func=AF.Copy)
        return lambda e: e.tensor_copy(out=out, in_=in_)

    def mm_group_a(wv, wkey, cols, act_ap, act_keys, nk, b, T):
        def f(e):
            inst = None
            for kc in range(nk):
                inst = e.matmul(bank(b)[:, 0:T], lhsT=wv[:, kc, cols], rhs=act_ap[:, kc, 0:T],
                                start=(kc == 0), stop=(kc == nk - 1))
            return inst
        S.add("pe", f, reads=[wkey] + list(act_keys), writes=[pk(b)])

    def norm_phase(t, gcol, alt=False):
        T = t["T"]; ntb = T // 128
        xs_b = xsG_b if alt else xs_b0
        junk_b = junkG_b if alt else junk_b0
        xs = xs_b.ap; junk = junk_b.ap; xnT = xnT_b.ap
        for tb in range(ntb):
            S.add("act", lambda e, tb=tb: e.activation(out=junk[:, :], in_=x_sb[:, tb, :], func=AF.Square,
                                                       accum_out=ss[:, tb:tb + 1]),
                  reads=[("x", tb)], writes=[junk_b.k(), ("ss", tb)])
            S.add("act", lambda e, tb=tb: e.activation(out=sd[:, tb:tb + 1], in_=ss[:, tb:tb + 1], func=AF.Sqrt,
                                                       scale=1.0 / D, bias=EPS),
                  reads=[("ss", tb)], writes=[("sd", tb)])
            S.add("dve", lambda e, tb=tb: e.reciprocal(out=rstd[:, tb:tb + 1], in_=sd[:, tb:tb + 1]),
                  reads=[("sd", tb)], writes=[("rstd", tb)])
            S.add("dve", lambda e, tb=tb: e.tensor_scalar_mul(out=xs[:, tb, :], in0=x_sb[:, tb, :],
                                                              scalar1=rstd[:, tb:tb + 1]),
                  reads=[("x", tb), ("rstd", tb)], writes=[xs_b.k(tb)])
        for kc in range(16):
            b = kc % 8

            def f(e, kc=kc, b=b):
                inst = None
                for tb in range(ntb):
                    inst = e.transpose(bank(b)[:, tb * 128:(tb + 1) * 128], xs[:, tb, kc * 128:(kc + 1) * 128], ident[:, :])
                return inst
            S.add("pe", f, reads=[xs_b.k(tb) for tb in range(ntb)] + [("ident",)], writes=[pk(b)])
            eng = evac_eng()
            if eng == "act":
                fn = lambda e, kc=kc, b=b: e.activation(out=xnT[:, kc, 0:T], in_=bank(b)[:, 0:T], func=AF.Copy,
                                                        scale=vecT[:, kc, gcol:gcol + 1])
            else:
                fn = lambda e, kc=kc, b=b: e.tensor_scalar_mul(out=xnT[:, kc, 0:T], in0=bank(b)[:, 0:T],
                                                               scalar1=vecT[:, kc, gcol:gcol + 1])
            S.add(eng, fn, reads=[pk(b), ("vecT",)], writes=[xnT_b.k(kc)])

    def post_norm(t, pr, gvec_d, store):
        T = t["T"]
        tF = tF_b.ap; gbc = gbc_b.ap
        for tl in range(2):
            tb = pr * 2 + tl
            psrow = PS[tl][:, :]
            pkeys = [pk(tl * 4 + c) for c in range(4)]
            S.add("act", lambda e, psrow=psrow, tl=tl: e.activation(out=tF[:, :], in_=psrow, func=AF.Square,
                                                                    accum_out=ss[:, 4 + tl:5 + tl]),
                  reads=pkeys, writes=[tF_b.k(), ("ss", 4 + tl)])
            S.add("act", lambda e, tl=tl: e.activation(out=sd[:, 4 + tl:5 + tl], in_=ss[:, 4 + tl:5 + tl], func=AF.Sqrt,
                                                       scale=1.0 / D, bias=EPS),
                  reads=[("ss", 4 + tl)], writes=[("sd", 4 + tl)])
            S.add("dve", lambda e, tl=tl: e.reciprocal(out=rstd[:, 4 + tl:5 + tl], in_=sd[:, 4 + tl:5 + tl]),
                  reads=[("sd", 4 + tl)], writes=[("rstd", 4 + tl)])
            S.add("dve", lambda e, psrow=psrow, tl=tl: e.scalar_tensor_tensor(out=tF[:, :], in0=psrow,
                                                                              scalar=rstd[:, 4 + tl:5 + tl], in1=gbc[:, :],
                                                                              op0=ALU.mult, op1=ALU.mult),
                  reads=pkeys + [("rstd", 4 + tl), gbc_b.k()], writes=[tF_b.k()])
            S.add("pool", lambda e, tb=tb: e.tensor_tensor(out=x_sb[:, tb, :], in0=x_sb[:, tb, :], in1=tF[:, :], op=ALU.add),
                  reads=[tF_b.k(), ("x", tb)], writes=[("x", tb)])
            if store is not None:
                dst = store[tb * 128:(tb + 1) * 128, :]
                S.add("pool", lambda e, dst=dst, tb=tb: e.dma_start(out=dst, in_=x_sb[:, tb, :]),
                      reads=[("x", tb)], writes=[("yout", tb)], dma=True)

    def load_gbc(gvec_d):
        gbc = gbc_b.ap
        S.add("sp", lambda e: e.dma_start(out=gbc[:, :], in_=AP(gvec_d, 0, [[0, 128], [1, D]])),
              writes=[gbc_b.k()], dma=True)

    def do_tile(t):
        T = t["T"]; ntb = T // 128; nseg = t["nseg"]; L = t["L"]
        is_p = t["kind"] == "p"
        ti = t["idx"]
        slot = (ti % 2) if is_p else 1
        xnT = xnT_b.ap; qT = qT_b.ap; oT = oT_b.ap; obT = obT_b.ap; mT = mT_b.ap; hid = hid_b.ap
        xsrc = x_p.ap()[ti * 512:(ti + 1) * 512, :] if is_p else x_s.ap()
        ydst = y_p.ap()[ti * 512:(ti + 1) * 512, :] if is_p else y_s.ap()

        for tb in range(ntb):
            S.add("sp", lambda e, xsrc=xsrc, tb=tb: e.dma_start(out=x_sb[:, tb, :], in_=xsrc[tb * 128:(tb + 1) * 128, :]),
                  writes=[("x", tb)], dma=True)
        norm_phase(t, V_PREG)

        if kstop == 'A':
            halted[0] = True
            return
        bctr = 0
        for i in range(4):
            wv, wk = W.next("q%d" % i)
            for j in range(2):
                f = i * 2 + j
                b = bctr % 8; bctr += 1
                mm_group_a(wv, wk, slice(j * 128, (j + 1) * 128), xnT, xnT_b.keys(), 16, b, T)
                eng = evac_eng()
                S.add(eng, copy_op(eng, qT[:, f, 0:T], bank(b)[:, 0:T]), reads=[pk(b)], writes=[qT_b.k(f)])
        if kstop == 'B1':
            halted[0] = True
            return
        for i in range(4):
            wv, wk = W.next("k%d" % i)
            for j in range(2):
                f = i * 2 + j
                b = bctr % 8; bctr += 1
                mm_group_a(wv, wk, slice(j * 128, (j + 1) * 128), xnT, xnT_b.keys(), 16, b, T)
                eng = evac_eng()
                S.add(eng, copy_op(eng, kT[:, f, slot * 512: slot * 512 + T], bank(b)[:, 0:T]),
                      reads=[pk(b)], writes=[("kT", slot, f)])

        if kstop == 'B2':
            halted[0] = True
            return
        def tokmajor_proj(prefix, to_vring, out_d):
            ngrp = 4
            M = 128 if is_p else 64
            for cg in range(2):
                for kh in range(2):
                    wv, wk = W.next("%s%d%d" % (prefix, cg, kh))
                    for g in range(ngrp):
                        b = g + (4 if cg else 0)

                        def f(e, wv=wv, g=g, b=b, kh=kh):
                            inst = None
                            for kc in range(8):
                                inst = e.matmul(bank(b)[0:M, :], lhsT=xnT[:, kh * 8 + kc, g * M:(g + 1) * M],
                                                rhs=wv[:, kc, :], start=(kh == 0 and kc == 0), stop=(kh == 1 and kc == 7))
                            return inst
                        S.add("pe", f, reads=[wk] + xnT_b.keys(), writes=[pk(b)])
                for g in range(ngrp):
                    b = g + (4 if cg else 0)
                    if to_vring:
                        blk = slot * 4 + g
                        eng = evac_eng()
                        dstv = Vr[0:M, blk, cg * 8:(cg + 1) * 8, 0:64]
                        srcv = bank(b)[0:M, :].rearrange("p (h d) -> p h d", h=8)
                        S.add(eng, copy_op(eng, dstv, srcv), reads=[pk(b), ("Vones",)], writes=[("V", blk, cg)])
                    if out_d is not None:
                        st = vst_b.ap
                        sidx = (g + cg) % 2
                        if not to_vring:
                            eng = evac_eng()
                        S.add(eng, copy_op(eng, st[0:M, sidx, :], bank(b)[0:M, :]), reads=[pk(b)], writes=[vst_b.k(sidx)])
                        dst = out_d[g * M:(g + 1) * M, cg * 512:(cg + 1) * 512]
                        kv = os.environ.get('KV', '')
                        if kv == 'nodma':
                            continue
                        srcst = st[0:M, sidx, :] if kv != 'xsrc' else x_sb[0:M, 0, 0:512]
                        S.add(os.environ.get('KQ', 'pool'), lambda e, dst=dst, srcst=srcst: e.dma_start(out=dst, in_=srcst),
                              reads=[vst_b.k(sidx)], writes=[("kvout", prefix, g, cg)], dma=True)

        if t["kv_out"]:
            vout = v_p.ap() if is_p else v_s.ap()
            kout = k_p.ap() if is_p else k_s.ap()
        else:
            vout = kout = None
        tokmajor_proj("v", True, vout if kstop != 'B3n' else None)
        if kstop in ('B3', 'B3n'):
            halted[0] = True
            return
        if t["kv_out"]:
            tokmajor_proj("kb", False, kout)

        if kstop == 'B':
            halted[0] = True
            return
        EB = EB_b.ap
        S.add("sp", lambda e: e.dma_start(out=EB[:, :, :].rearrange("p h n -> p (h n)"), in_=ebd.ap()),
              reads=[("ebd",)], writes=EB_b.keys(), dma=True)
        P = P_b.ap; osb = osb_b.ap
        SB_ = [PS[0][:, 0:1024], PS[0][:, 1024:2048]]
        SBK = [[pk(0), pk(1)], [pk(2), pk(3)]]
        OBS = [bank(4), bank(5)]
        nqb = 4
        for qb in range(nqb):
            if is_p:
                NQ = 128
                qc0 = qb * 128
                G = ti * 4 + qb
                kbl = []
                for jp in range(5):
                    gb = G - jp
                    if gb < 0:
                        continue
                    rs = (gb // 4) % 2
                    kbl.append(dict(jp=jp, nk=128, kcol=rs * 512 + (gb % 4) * 128, vblk=rs * 4 + (gb % 4),
                                    kkey_slot=rs))
            else:
                NQ = 64
                qc0 = qb * 64
                s = qb
                ckst = ckst_b.ap
                for half in range(2):
                    S.add("sp", lambda e, s=s, half=half: e.dma_start(
                        out=ckst[:, :, :], in_=ck_d.ap()[s, half * 256:(half + 1) * 256, :].rearrange("(b p) d -> p b d", p=128)),
                        writes=ckst_b.keys(), dma=True)
                    for c in range(8):
                        b = 6 + (c % 2)

                        def f(e, c=c, b=b):
                            inst = None
                            for bl in range(2):
                                inst = e.transpose(bank(b)[:, bl * 128:(bl + 1) * 128], ckst[:, bl, c * 128:(c + 1) * 128], ident[:, :])
                            return inst
                        S.add("pe", f, reads=ckst_b.keys() + [("ident",)], writes=[pk(b)])
                        eng = evac_eng()
                        S.add(eng, copy_op(eng, kT[:, c, half * 256:(half + 1) * 256], bank(b)[:, 0:256]),
                              reads=[pk(b)], writes=[("kT", 0, c)])
                for blk in range(4):
                    S.add("pool", lambda e, s=s, blk=blk: e.dma_start(
                        out=Vr[:, blk, :, 0:64], in_=cv_d.ap()[s, blk * 128:(blk + 1) * 128, :].rearrange("p (h d) -> p h d", h=NH)),
                        reads=[("Vones",)], writes=[("V", blk, cg) for cg in range(2)], dma=True)
                kbl = [dict(jp=0, nk=64, kcol=512 + s * 64, vblk=4 + s, kkey_slot=1)]
                for jp in range(1, 5):
                    cb = 4 - jp
                    kbl.append(dict(jp=jp, nk=128, kcol=cb * 128, vblk=cb, kkey_slot=0))
            njp = max(kb["jp"] for kb in kbl) + 1
            def attn_S(h):
                hc = h // 2; hp = h % 2
                si = h % 2
                Sps = SB_[si]

                def fS(e, kbl=kbl, hc=hc, hp=hp, Sps=Sps, qc0=qc0, NQ=NQ, h=h):
                    inst = None
                    ncols = njp * 128
                    c = 0
                    while c < ncols:
                        w = min(512, ncols - c)
                        e.matmul(Sps[:, c:c + w], lhsT=identb[:, :], rhs=EB[:, h, c:c + w], start=True, stop=False,
                                 skip_group_check=True)
                        c += w
                    for i, kb in enumerate(kbl):
                        inst = e.matmul(Sps[0:kb["nk"], kb["jp"] * 128: kb["jp"] * 128 + NQ],
                                        lhsT=kT[hp * 64:(hp + 1) * 64, hc, kb["kcol"]: kb["kcol"] + kb["nk"]],
                                        rhs=qT[hp * 64:(hp + 1) * 64, hc, qc0:qc0 + NQ], start=False,
                                        stop=(i == len(kbl) - 1), skip_group_check=True)
                    return inst
                S.add("pe", fS, reads=[("kT", kb["kkey_slot"], hc) for kb in kbl] + [qT_b.k(hc), EB_b.k(h), ("identb",)],
                      writes=SBK[si])
            def attn_rest(h):
                hc = h // 2; hp = h % 2
                si = h % 2
                Sps = SB_[si]
                pi = h % 4
                Sv = Sps[:, 0:njp * 128].rearrange("p (j q) -> p j q", j=njp)[:, :, 0:NQ]
                Pv = P[:, pi, 0:njp * 128].rearrange("p (j q) -> p j q", j=njp)[:, :, 0:NQ]
                S.add("act", lambda e, Sv=Sv, Pv=Pv: e.activation(out=Pv, in_=Sv, func=AF.Exp, scale=0.125),
                      reads=SBK[si], writes=[P_b.k(pi)])
                osl = h % 2
                OB = OBS[osl]

                def fO(e, kbl=kbl, h=h, si=pi, OB=OB, NQ=NQ):
                    inst = None
                    for n, kb in enumerate(kbl):
                        inst = e.matmul(OB[0:NQ, 0:65],
                                        lhsT=P[0:kb["nk"], si, kb["jp"] * 128: kb["jp"] * 128 + NQ],
                                        rhs=Vr[0:kb["nk"], kb["vblk"], h, :], start=(n == 0), stop=(n == len(kbl) - 1))
                    return inst
                S.add("pe", fO, reads=[P_b.k(pi), ("Vones",)] + [("V", kb["vblk"], h // 8) for kb in kbl],
                      writes=[pk(4 + osl)])
                S.add("dve", lambda e, osl=osl, NQ=NQ, OB=OB: e.reciprocal(out=rc[0:NQ, osl:osl + 1], in_=OB[0:NQ, 64:65]),
                      reads=[pk(4 + osl)], writes=[("rc", osl)])
                S.add("dve", lambda e, osl=osl, h=h, NQ=NQ, OB=OB: e.tensor_scalar_mul(out=osb[0:NQ, h * 64:(h + 1) * 64],
                                                                              in0=OB[0:NQ, 0:64],
                                                                              scalar1=rc[0:NQ, osl:osl + 1]),
                      reads=[pk(4 + osl), ("rc", osl)], writes=[("osb", h)])
            attn_S(0)
            for h in range(NH):
                if h + 1 < NH:
                    attn_S(h + 1)
                attn_rest(h)
            for half in range(2):
                b = 6 + half

                def fT(e, half=half, b=b, NQ=NQ):
                    inst = None
                    for c4 in range(4):
                        c = half * 4 + c4
                        inst = e.transpose(bank(b)[:, c4 * 128: c4 * 128 + NQ], osb[0:NQ, c * 128:(c + 1) * 128], ident[0:NQ, 0:NQ])
                    return inst
                S.add("pe", fT, reads=[("osb", hh) for hh in range(NH)] + [("ident",)], writes=[pk(b)])
                eng = evac_eng()
                srcv = bank(b)[:, :].rearrange("p (c q) -> p c q", c=4)[:, :, 0:NQ]
                dstv = oT[:, half * 4:(half + 1) * 4, qc0:qc0 + NQ]
                S.add(eng, copy_op(eng, dstv, srcv), reads=[pk(b)], writes=[oT_b.k(half * 4 + c4) for c4 in range(4)])

        if kstop == 'C':
            halted[0] = True
            return
        gv = gbuf[:, :].rearrange("p (k n) -> p k n", k=16); gk = ("gbuf",)
        xe = xe_b.ap; xc = xc_b.ap; xcb = xcb_b.ap; thr = thr_b.ap; a_ = a_b.ap; a2 = a2_b.ap
        thi = thi_b.ap; u_ = u_b.ap; h_ = h_b.ap; gl = gl_b.ap
        SEGW = 3 + L
        xe3 = xe[:, 0:nseg * SEGW].rearrange("p (s w) -> p s w", s=nseg)

        def v3(ap2d):
            return ap2d.rearrange("p (s w) -> p s w", s=nseg)
        rw = {}
        xc2_b = RB("xc2", R, "R", 8 * KB, [512], F32)
        XC = [xc_b, xc2_b]

        def rnn_mm_xr(n):
            if n % 2 == 0:
                rw["xr"] = W.next("xr%d" % (n // 2))
            j = n % 2
            mm_group_a(rw["xr"][0], rw["xr"][1], slice(j * 128, (j + 1) * 128), xnT, xnT_b.keys(), 16, n % 2, T)

        def rnn_mm_gr(n):
            if n % 2 == 0:
                rw["gr"] = W.next("gr%d" % (n // 2))
            j = n % 2
            mm_group_a(rw["gr"][0], rw["gr"][1], slice(j * 128, (j + 1) * 128), xnT, xnT_b.keys(), 16, 6 + n % 2, T)

        def rnn_s1(n):
            bxr = n % 2
            br, bi = 2 + 2 * (n % 2), 3 + 2 * (n % 2)
            xcn_b = XC[n % 2]
            xcn = xcn_b.ap
            if is_p:
                S.add("pool", lambda e, n=n: e.tensor_copy(out=xe[:, 0:3], in_=hist[:, n, 0:3]),
                      reads=[("hist", n)], writes=[xe_b.k()])
            else:
                S.add("pool", lambda e, n=n: e.tensor_copy(out=xe3[:, :, 0:3],
                                                           in_=stT[:, n, 0:12].rearrange("p (s j) -> p s j", s=4)),
                      reads=[("stT",)], writes=[xe_b.k()])
            S.add("act", lambda e, bxr=bxr: e.activation(out=xe3[:, :, 3:3 + L], in_=v3(bank(bxr)[:, 0:T]), func=AF.Copy),
                  reads=[pk(bxr)], writes=[xe_b.k()])
            S.add("act", lambda e, bxr=bxr, n=n: e.activation(out=xcn[:, 0:T], in_=bank(bxr)[:, 0:T], func=AF.Identity,
                                                             scale=vecT[:, n, V_CW0 + 3:V_CW0 + 4], bias=vecT[:, n, V_CB:V_CB + 1]),
                  reads=[pk(bxr), ("vecT",)], writes=[xcn_b.k()])
            for jt in range(3):
                S.add("dve", lambda e, jt=jt, n=n: e.scalar_tensor_tensor(out=v3(xcn[:, 0:T]), in0=xe3[:, :, jt:jt + L],
                                                                         scalar=vecT[:, n, V_CW0 + jt:V_CW0 + jt + 1],
                                                                         in1=v3(xcn[:, 0:T]), op0=ALU.mult, op1=ALU.add),
                      reads=[xe_b.k(), xcn_b.k(), ("vecT",)], writes=[xcn_b.k()])
            if is_p:
                S.add("pool", lambda e, n=n: e.tensor_copy(out=hist[:, n, 0:3], in_=xe[:, L:L + 3]),
                      reads=[xe_b.k()], writes=[("hist", n)])
                if ti == n_pt - 1:
                    S.add("pool", lambda e, n=n: e.tensor_copy(out=outst[:, n, 0:3], in_=xe[:, L:L + 3]),
                          reads=[xe_b.k()], writes=[("outst", n, 0)])
            else:
                S.add("pool", lambda e, n=n: e.tensor_copy(out=outst[:, n, 0:12].rearrange("p (s j) -> p s j", s=4),
                                                           in_=xe3[:, :, L:L + 3]),
                      reads=[xe_b.k()], writes=[("outst", n, 0)])
            S.add("dve", lambda e: e.tensor_copy(out=xcb[:, 0:T], in_=xcn[:, 0:T]), reads=[xcn_b.k()], writes=[xcb_b.k()])
            S.add("pe", lambda e, n=n, br=br: e.matmul(bank(br)[:, 0:T], lhsT=gv[:, n, 0:128], rhs=xcb[:, 0:T], start=True, stop=True),
                  reads=[gk, xcb_b.k()], writes=[pk(br)])
            S.add("pe", lambda e, n=n, bi=bi: e.matmul(bank(bi)[:, 0:T], lhsT=gv[:, n, 128:256], rhs=xcb[:, 0:T], start=True, stop=True),
                  reads=[gk, xcb_b.k()], writes=[pk(bi)])

        def rnn_s2(n):
            br, bi = 2 + 2 * (n % 2), 3 + 2 * (n % 2)
            bgr = 6 + n % 2
            xcn_b = XC[n % 2]
            xcn = xcn_b.ap
            S.add("act", lambda e, n=n, br=br: e.activation(out=thr[:, 0:T], in_=bank(br)[:, 0:T], func=AF.Tanh, scale=0.5,
                                                           bias=dT[:, n, 0:1]),
                  reads=[pk(br)] + DTK, writes=[thr_b.k()])
            S.add("act", lambda e, n=n: e.activation(out=a_[:, 0:T], in_=thr[:, 0:T], func=AF.Exp, scale=dT[:, n, 3:4], bias=dT[:, n, 3:4]),
                  reads=[thr_b.k()] + DTK, writes=[a_b.k()])
            S.add("dve", lambda e: e.tensor_tensor(out=a2[:, 0:T], in0=a_[:, 0:T], in1=a_[:, 0:T], op=ALU.mult),
                  reads=[a_b.k()], writes=[a2_b.k()])
            S.add("act", lambda e, n=n, bi=bi: e.activation(out=thi[:, 0:T], in_=bank(bi)[:, 0:T], func=AF.Tanh, scale=0.5,
                                                           bias=dT[:, n, 1:2]),
                  reads=[pk(bi)] + DTK, writes=[thi_b.k()])
            S.add("act", lambda e: e.activation(out=a2[:, 0:T], in_=a2[:, 0:T], func=AF.Sqrt, scale=-1.0, bias=1.0),
                  reads=[a2_b.k()], writes=[a2_b.k()])
            S.add("dve", lambda e: e.scalar_tensor_tensor(out=u_[:, 0:T], in0=thi[:, 0:T], scalar=1.0, in1=xcn[:, 0:T],
                                                          op0=ALU.add, op1=ALU.mult),
                  reads=[thi_b.k(), xcn_b.k()], writes=[u_b.k()])
            S.add("dve", lambda e: e.scalar_tensor_tensor(out=u_[:, 0:T], in0=u_[:, 0:T], scalar=0.5, in1=a2[:, 0:T],
                                                          op0=ALU.mult, op1=ALU.mult),
                  reads=[u_b.k(), a2_b.k()], writes=[u_b.k()])
            for sgi in range(nseg):
                if is_p:
                    init = hlast[:, n, 0:1]
                    ikey = ("hlast", n)
                else:
                    init = stT[:, n, 12 + sgi:13 + sgi]
                    ikey = ("stT",)
                S.add("dve", lambda e, sgi=sgi, init=init: e.tensor_tensor_scan(
                    out=h_[:, sgi * L:(sgi + 1) * L], data0=a_[:, sgi * L:(sgi + 1) * L], data1=u_[:, sgi * L:(sgi + 1) * L],
                    initial=init, op0=ALU.mult, op1=ALU.add),
                    reads=[a_b.k(), u_b.k(), ikey], writes=[h_b.k()])
            if is_p:
                S.add("pool", lambda e, n=n: e.tensor_copy(out=hlast[:, n, 0:1], in_=h_[:, L - 1:L]),
                      reads=[h_b.k()], writes=[("hlast", n)])
                if ti == n_pt - 1:
                    S.add("pool", lambda e, n=n: e.tensor_copy(out=outst[:, n, 3:4], in_=h_[:, L - 1:L]),
                          reads=[h_b.k()], writes=[("outst", n, 1)])
            else:
                S.add("pool", lambda e, n=n: e.tensor_copy(out=outst[:, n, 12:16],
                                                           in_=v3(h_[:, 0:T])[:, :, L - 1]),
                      reads=[h_b.k()], writes=[("outst", n, 1)])
            S.add("act", lambda e, bgr=bgr: e.activation(out=gl[:, 0:T], in_=bank(bgr)[:, 0:T], func=AF.Gelu_apprx_tanh),
                  reads=[pk(bgr)], writes=[gl_b.k()])
            S.add("dve", lambda e, n=n: e.tensor_tensor(out=obT[:, n, 0:T], in0=h_[:, 0:T], in1=gl[:, 0:T], op=ALU.mult),
                  reads=[h_b.k(), gl_b.k()], writes=[obT_b.k(n)])

        rnn_mm_xr(0)
        rnn_s1(0)
        for n in range(16):
            if n + 1 < 16:
                rnn_mm_xr(n + 1)
            rnn_mm_gr(n)
            if n + 1 < 16:
                rnn_s1(n + 1)
            rnn_s2(n)

        if (is_p and ti == n_pt - 1) or not is_p:
            ncol = 4 if is_p else 16
            okeys = [("outst", n, c) for n in range(16) for c in range(2)]
            for grp in range(4):
                b = 4 + grp % 2

                def fo(e, grp=grp, b=b, ncol=ncol):
                    inst = None
                    for c4 in range(4):
                        kc = grp * 4 + c4
                        inst = e.transpose(bank(b)[0:ncol, c4 * 128:(c4 + 1) * 128], outst[:, kc, 0:ncol], ident[:, :])
                    return inst
                S.add("pe", fo, reads=okeys + [("ident",)], writes=[pk(b)])
                S.add("dve", lambda e, grp=grp, b=b, ncol=ncol: e.tensor_copy(out=outrow[0:ncol, grp * 512:(grp + 1) * 512],
                                                                            in_=bank(b)[0:ncol, :]),
                      reads=[pk(b)], writes=[outrow_b.k()])
            rk = [outrow_b.k()]
            if is_p:
                S.add("pool", lambda e: e.dma_start(out=conv_p.ap(), in_=outrow[0:3, :]), reads=rk, writes=[("o_conv",)], dma=True)
                S.add("pool", lambda e: e.dma_start(out=h_p.ap(), in_=outrow[3:4, :]), reads=rk, writes=[("o_h",)], dma=True)
            else:
                S.add("pool", lambda e: e.dma_start(out=conv_s.ap(), in_=outrow[0:12, :]), reads=rk, writes=[("o_conv",)], dma=True)
                S.add("pool", lambda e: e.dma_start(out=h_s.ap(), in_=outrow[12:16, :]), reads=rk, writes=[("o_h",)], dma=True)

        if kstop == 'D':
            halted[0] = True
            return
        sg = sg_b.ap; m1 = m1_b.ap
        for i in range(8):
            gav, gak = W.next("ga%d" % i)
            for j in range(2):
                mm_group_a(gav, gak, slice(j * 128, (j + 1) * 128), xnT, xnT_b.keys(), 16, j, T)
                S.add("act", lambda e, j=j: e.activation(out=sg[:, j, 0:T], in_=bank(j)[:, 0:T], func=AF.Sigmoid),
                      reads=[pk(j)], writes=[sg_b.k(j)])
            gbv, gbk = W.next("gb%d" % i)
            for j in range(2):
                mm_group_a(gbv, gbk, slice(j * 128, (j + 1) * 128), xnT, xnT_b.keys(), 16, 2 + j, T)
                S.add("act", lambda e, j=j: e.activation(out=sg[:, 2 + j, 0:T], in_=bank(2 + j)[:, 0:T], func=AF.Sigmoid),
                      reads=[pk(2 + j)], writes=[sg_b.k(2 + j)])
            auv, auk = W.next("au%d" % i)
            for j in range(2):
                mm_group_a(auv, auk, slice(j * 128, (j + 1) * 128), oT, oT_b.keys(), 8, 4 + j, T)
                S.add("dve", lambda e, j=j: e.tensor_tensor(out=m1[:, j, 0:T], in0=sg[:, j, 0:T], in1=bank(4 + j)[:, 0:T], op=ALU.mult),
                      reads=[sg_b.k(j), pk(4 + j)], writes=[m1_b.k(j)])
            ruv, ruk = W.next("ru%d" % i)
            for j in range(2):
                f = i * 2 + j
                mm_group_a(ruv, ruk, slice(j * 128, (j + 1) * 128), obT, obT_b.keys(), 16, 6 + j, T)
                S.add("dve", lambda e, j=j: e.tensor_tensor(out=sg[:, 2 + j, 0:T], in0=sg[:, 2 + j, 0:T], in1=bank(6 + j)[:, 0:T], op=ALU.mult),
                      reads=[sg_b.k(2 + j), pk(6 + j)], writes=[sg_b.k(2 + j)])
                S.add("pool", lambda e, j=j, f=f: e.tensor_tensor(out=mT[:, f, 0:T], in0=m1[:, j, 0:T], in1=sg[:, 2 + j, 0:T], op=ALU.add),
                      reads=[m1_b.k(j), sg_b.k(2 + j)], writes=[mT_b.k(f)])

        if kstop == 'E':
            halted[0] = True
            return
        load_gbc(post_mix_g_d)
        npair = T // 256
        for pr in range(npair):
            for cg in range(4):
                for kh in range(2):
                    wv, wk = W.next("wo%d%d" % (cg, kh))
                    for tl in range(2):
                        tb = pr * 2 + tl
                        b = tl * 4 + cg

                        def f(e, wv=wv, kh=kh, tb=tb, b=b):
                            inst = None
                            for kc in range(8):
                                inst = e.matmul(bank(b)[:, :], lhsT=mT[:, kh * 8 + kc, tb * 128:(tb + 1) * 128], rhs=wv[:, kc, :],
                                                start=(kh == 0 and kc == 0), stop=(kh == 1 and kc == 7))
                            return inst
                        S.add("pe", f, reads=[wk] + mT_b.keys(), writes=[pk(b)])
            post_norm(t, pr, post_mix_g_d, None)

        if kstop == 'F':
            halted[0] = True
            return
        norm_phase(t, V_FFNG, alt=True)

        if kstop == 'G':
            halted[0] = True
            return
        for i in range(32):
            wv, wk = W.next("f1_%d" % i)
            for j in range(2):
                f = i * 2 + j
                b = f % 8
                mm_group_a(wv, wk, slice(j * 128, (j + 1) * 128), xnT, xnT_b.keys(), 16, b, T)
                ri = f % 2
                S.add("act", lambda e, b=b, ri=ri: e.activation(out=rl[:, ri, 0:T], in_=bank(b)[:, 0:T], func=AF.Relu),
                      reads=[pk(b)], writes=[("rl", ri)])
                S.add("dve", lambda e, b=b, ri=ri, f=f: e.tensor_tensor(out=hid[:, f, 0:T], in0=bank(b)[:, 0:T], in1=rl[:, ri, 0:T], op=ALU.mult),
                      reads=[pk(b), ("rl", ri)], writes=[hid_b.k(f)])

        if kstop == 'H':
            halted[0] = True
            return
        load_gbc(post_ffn_g_d)
        for pr in range(npair):
            for cg in range(4):
                for pc in range(8):
                    wv, wk = W.next("f2_%d%d" % (cg, pc))
                    for tl in range(2):
                        tb = pr * 2 + tl
                        b = tl * 4 + cg

                        def f(e, wv=wv, pc=pc, tb=tb, b=b):
                            inst = None
                            for kc in range(8):
                                inst = e.matmul(bank(b)[:, :], lhsT=hid[:, pc * 8 + kc, tb * 128:(tb + 1) * 128], rhs=wv[:, kc, :],
                                                start=(pc == 0 and kc == 0), stop=(pc == 7 and kc == 7))
                            return inst
                        S.add("pe", f, reads=[wk] + hid_b.keys(), writes=[pk(b)])
            post_norm(t, pr, post_ffn_g_d, ydst)

    for t in tiles:
        if kstop == 'pro' or halted[0]:
            break
        do_tile(t)

    assert kstop or W.pos == len(stream)
    S.emit(nc, es)
    es.close()
    return nc


_NC_CACHE = {}


def _get_nc(n_pt):
    if n_pt not in _NC_CACHE:
        _NC_CACHE[n_pt] = build_program(n_pt)
    return _NC_CACHE[n_pt]


def kernel(x_prompt, x_sample, cache_k, cache_v, state_conv, state_h,
           pre_mix_g, w_in, rel_bias, conv_w, conv_b, w_rg, b_rg, w_ig, b_ig, lru_lambda,
           w_attn_up, w_rnn_up, w_out, post_mix_g, pre_ffn_g, w_ff1, w_ff2, post_ffn_g):
    f = lambda a: np.ascontiguousarray(np.asarray(a, dtype=np.float32))
    x_prompt = f(x_prompt); x_sample = f(x_sample)
    B, SEQ, _ = x_prompt.shape
    n_pt = SEQ // 512
    ncores = 8
    assert B == ncores and x_sample.shape[0] == 4 * ncores
    vecs = np.concatenate([f(pre_mix_g)[0][None], f(pre_ffn_g)[0][None], f(conv_w)[0], f(conv_b)[0][None],
                           f(b_rg)[0][None], f(b_ig)[0][None], f(lru_lambda)[0][None]], axis=0)
    vecs = np.ascontiguousarray(vecs)
    shared = {
        "vecs": vecs, "post_mix_g": f(post_mix_g)[0][None].copy(), "post_ffn_g": f(post_ffn_g)[0][None].copy(),
        "w_in": f(w_in)[0], "rel_bias": f(rel_bias)[0], "w_rg": f(w_rg)[0], "w_ig": f(w_ig)[0],
        "w_attn_up": f(w_attn_up)[0], "w_rnn_up": f(w_rnn_up)[0], "w_out": f(w_out)[0],
        "w_ff1": f(w_ff1)[0], "w_ff2": f(w_ff2)[0],
    }
    ck = f(cache_k)[0].reshape(32, 512, DA); cv = f(cache_v)[0].reshape(32, 512, DA)
    sc = f(state_conv)[0]; sh = f(state_h)[0]
    in_maps = []
    for c in range(ncores):
        m = dict(shared)
        m["x_p"] = x_prompt[c]
        m["x_s"] = np.ascontiguousarray(x_sample[4 * c:4 * c + 4].reshape(256, D))
        m["ck"] = np.ascontiguousarray(ck[4 * c:4 * c + 4])
        m["cv"] = np.ascontiguousarray(cv[4 * c:4 * c + 4])
        m["sconv"] = np.ascontiguousarray(sc[4 * c:4 * c + 4].reshape(12, D))
        m["sh"] = np.ascontiguousarray(sh[4 * c:4 * c + 4])
        in_maps.append(m)
    nc = _get_nc(n_pt)
    res = run_bass_kernel_spmd(nc, in_maps, core_ids=list(range(ncores)))
    r = res.results
    y_prompt = np.stack([r[c]["y_p"] for c in range(ncores)], 0)
    y_sample = np.concatenate([r[c]["y_s"].reshape(4, 64, D) for c in range(ncores)], 0)
    k_prompt = np.stack([r[c]["k_p"].reshape(512, NH, DH) for c in range(ncores)], 0)[None]
    v_prompt = np.stack([r[c]["v_p"].reshape(512, NH, DH) for c in range(ncores)], 0)[None]
    conv_prompt = np.stack([r[c]["conv_p"] for c in range(ncores)], 0)[None]
    h_prompt = np.stack([r[c]["h_p"][0] for c in range(ncores)], 0)[None]
    k_sample = np.concatenate([r[c]["k_s"].reshape(4, 64, NH, DH) for c in range(ncores)], 0)[None]
    v_sample = np.concatenate([r[c]["v_s"].reshape(4, 64, NH, DH) for c in range(ncores)], 0)[None]
    conv_sample = np.concatenate([r[c]["conv_s"].reshape(4, 3, D) for c in range(ncores)], 0)[None]
    h_sample = np.concatenate([r[c]["h_s"] for c in range(ncores)], 0)[None]
    outs = (y_prompt, y_sample, k_prompt, v_prompt, conv_prompt, h_prompt,
            k_sample, v_sample, conv_sample, h_sample)
    return tuple(np.ascontiguousarray(o, dtype=np.float32) for o in outs)
```
